# Optimizing a Trainium2 kernel written in Bass

```python
import math
import jax, jax.numpy as jnp
from jax import lax
import numpy as np

D_MODEL = 1024
BATCH = 16
SEQ = 256
DEPTH = 1
DEC_BATCH = 8
DEC_SEQ = 4096
PAST_LEN = 256

GRID_W = 64
N_HEADS = 4
HEAD_DIM = 64
V_DIM = 2 * HEAD_DIM
QK_WIDTH = N_HEADS * 2 * HEAD_DIM
ATTN_WIDTH = N_HEADS * V_DIM
CONV_WIDTH = D_MODEL - ATTN_WIDTH
CONV_K = 3
IN_WIDTH = 2 * QK_WIDTH + ATTN_WIDTH + 3 * CONV_WIDTH
D_FF = -(-8 * D_MODEL // (3 * 256)) * 256
ROPE_THETA = 10000.0
Q_BLOCK = 128
EPS = 1e-6
ATTN_SCALE = HEAD_DIM ** -0.5

kernel_name = "hybrid_diffattn_shortconv_dit_step"


def rms_norm(x, g):
    xf = x.astype(jnp.float32)
    y = xf * lax.rsqrt(jnp.mean(xf * xf, axis=-1, keepdims=True) + EPS)
    return (y * g.astype(jnp.float32)).astype(x.dtype)


def modulation(cond, w_ada, b_ada):
    m = jax.nn.silu(cond) @ w_ada + b_ada
    return jnp.split(m, 6, axis=-1)


def project(x, shift, scale, g_pre, w_in):
    b, n = x.shape[0], x.shape[1]
    u = rms_norm(x, g_pre) * (1.0 + scale) + shift
    p = u @ w_in
    o = 0
    q = p[..., o:o + QK_WIDTH].reshape(b, n, N_HEADS, 2, HEAD_DIM); o += QK_WIDTH
    k = p[..., o:o + QK_WIDTH].reshape(b, n, N_HEADS, 2, HEAD_DIM); o += QK_WIDTH
    v = p[..., o:o + ATTN_WIDTH].reshape(b, n, N_HEADS, V_DIM); o += ATTN_WIDTH
    bg = p[..., o:o + CONV_WIDTH]; o += CONV_WIDTH
    cg = p[..., o:o + CONV_WIDTH]; o += CONV_WIDTH
    xi = p[..., o:o + CONV_WIDTH]
    return q, k, v, bg, cg, xi


def axial_rotary_tables(n):
    rows = n // GRID_W
    t = jnp.arange(rows * GRID_W)
    row = (t // GRID_W).astype(jnp.float32)
    col = (t % GRID_W).astype(jnp.float32)
    n_freq = HEAD_DIM // 4
    inv = 1.0 / (ROPE_THETA ** (jnp.arange(n_freq, dtype=jnp.float32) / n_freq))
    ar = row[:, None] * inv[None, :]
    ac = col[:, None] * inv[None, :]
    shp = (1, n, 1, 1, n_freq)
    return (jnp.cos(ar).reshape(shp), jnp.sin(ar).reshape(shp),
            jnp.cos(ac).reshape(shp), jnp.sin(ac).reshape(shp))


def _rotate(x, cos, sin):
    h = x.shape[-1] // 2
    x1, x2 = x[..., :h], x[..., h:]
    return jnp.concatenate([x1 * cos - x2 * sin, x2 * cos + x1 * sin], axis=-1)


def apply_axial_rotary(x, tables):
    cr, sr, cc, sc = tables
    xf = x.astype(jnp.float32)
    half = HEAD_DIM // 2
    out = jnp.concatenate([_rotate(xf[..., :half], cr, sr),
                           _rotate(xf[..., half:], cc, sc)], axis=-1)
    return out.astype(x.dtype)


def diff_attention(q, k, v, lam, g_subln, lambda_init):
    b, nq = q.shape[0], q.shape[1]
    nb = nq // Q_BLOCK
    qb = q.reshape(b, nb, Q_BLOCK, N_HEADS, 2, HEAD_DIM).swapaxes(0, 1)
    kf = k.astype(jnp.float32)
    vf = v.astype(jnp.float32)

    def block(qblk):
        s = jnp.einsum('bqhmd,bkhmd->bhmqk', qblk.astype(jnp.float32), kf) * ATTN_SCALE
        p = jax.nn.softmax(s, axis=-1)
        a = p[:, :, 0] - lam * p[:, :, 1]
        return jnp.einsum('bhqk,bkhe->bqhe', a, vf)

    o = lax.map(block, qb)
    o = o.swapaxes(0, 1).reshape(b, nq, N_HEADS, V_DIM)
    o = o * lax.rsqrt(jnp.mean(o * o, axis=-1, keepdims=True) + EPS)
    o = o * g_subln.astype(jnp.float32) * (1.0 - lambda_init)
    return o.reshape(b, nq, ATTN_WIDTH).astype(q.dtype)


def gated_short_conv(bg, cg, xi, conv_w, conv_b):
    z = cg * xi
    zp = jnp.pad(z, ((0, 0), (1, 1), (0, 0)))
    y = conv_w[0] * zp[:, :-2] + conv_w[1] * zp[:, 1:-1] + conv_w[2] * zp[:, 2:] + conv_b
    return bg * y


def finish_layer(x, ao, co, gate_a, shift_f, scale_f, gate_f,
                 w_out, g_attn_post, g_ffn_pre, g_ffn_post, w_ffn_in, w_ffn_out):
    mix = jnp.concatenate([ao, co], axis=-1) @ w_out
    x = x + gate_a * rms_norm(mix, g_attn_post)
    u = rms_norm(x, g_ffn_pre) * (1.0 + scale_f) + shift_f
    gu = u @ w_ffn_in
    h = jax.nn.silu(gu[..., :D_FF]) * gu[..., D_FF:]
    return x + gate_f * rms_norm(h @ w_ffn_out, g_ffn_post)


def setup_inputs(seed: int = 0) -> dict:
    key = jax.random.key(seed)
    ks = jax.random.split(key, 24)
    f32 = jnp.float32
    nrm = lambda k, s, sc: jax.random.normal(k, s, f32) * sc
    gain = lambda k, s: 1.0 + 0.1 * jax.random.normal(k, s, f32)
    return {
        "x_prompt": nrm(ks[0], (BATCH, SEQ, D_MODEL), 1.0),
        "x_sample": nrm(ks[1], (DEC_BATCH, DEC_SEQ, D_MODEL), 1.0),
        "cache_k": nrm(ks[2], (DEC_BATCH, DEPTH, PAST_LEN, N_HEADS, 2 * HEAD_DIM), 1.0),
        "cache_v": nrm(ks[3], (DEC_BATCH, DEPTH, PAST_LEN, N_HEADS, V_DIM), 1.0),
        "c": nrm(ks[4], (DEC_BATCH, D_MODEL), 1.0),
        "c_ctx": nrm(ks[5], (D_MODEL,), 1.0),
        "w_ada": nrm(ks[6], (DEPTH, D_MODEL, 6 * D_MODEL), 0.5 * D_MODEL ** -0.5),
        "b_ada": nrm(ks[7], (DEPTH, 6 * D_MODEL), 0.02),
        "g_attn_pre": gain(ks[8], (DEPTH, D_MODEL)),
        "g_attn_post": gain(ks[9], (DEPTH, D_MODEL)),
        "g_ffn_pre": gain(ks[10], (DEPTH, D_MODEL)),
        "g_ffn_post": gain(ks[11], (DEPTH, D_MODEL)),
        "w_in": nrm(ks[12], (DEPTH, D_MODEL, IN_WIDTH), D_MODEL ** -0.5),
        "conv_w": nrm(ks[13], (DEPTH, CONV_K, CONV_WIDTH), CONV_K ** -0.5),
        "conv_b": nrm(ks[14], (DEPTH, CONV_WIDTH), 0.02),
        "lambda_q1": nrm(ks[15], (DEPTH, HEAD_DIM), 0.1),
        "lambda_k1": nrm(ks[16], (DEPTH, HEAD_DIM), 0.1),
        "lambda_q2": nrm(ks[17], (DEPTH, HEAD_DIM), 0.1),
        "lambda_k2": nrm(ks[18], (DEPTH, HEAD_DIM), 0.1),
        "g_subln": gain(ks[19], (DEPTH, V_DIM)),
        "w_out": nrm(ks[20], (DEPTH, D_MODEL, D_MODEL), D_MODEL ** -0.5),
        "w_ffn_in": nrm(ks[21], (DEPTH, D_MODEL, 2 * D_FF), D_MODEL ** -0.5),
        "w_ffn_out": nrm(ks[22], (DEPTH, D_FF, D_MODEL), D_FF ** -0.5),
    }


def reference(x_prompt, x_sample, cache_k, cache_v, c, c_ctx, w_ada, b_ada,
              g_attn_pre, g_attn_post, g_ffn_pre, g_ffn_post, w_in, conv_w, conv_b,
              lambda_q1, lambda_k1, lambda_q2, lambda_k2, g_subln, w_out,
              w_ffn_in, w_ffn_out):
    bp, sp = x_prompt.shape[0], x_prompt.shape[1]
    bs, ns = x_sample.shape[0], x_sample.shape[1]
    tables = axial_rotary_tables(ns)
    xp, xs = x_prompt, x_sample
    new_k, new_v = [], []
    for l in range(DEPTH):
        lambda_init = 0.8 - 0.6 * math.exp(-0.3 * l)
        lam = (jnp.exp(jnp.sum(lambda_q1[l].astype(jnp.float32) * lambda_k1[l].astype(jnp.float32)))
               - jnp.exp(jnp.sum(lambda_q2[l].astype(jnp.float32) * lambda_k2[l].astype(jnp.float32)))
               + lambda_init)

        sa, ca, ga, sf, cf, gf = modulation(c_ctx, w_ada[l], b_ada[l])
        q, k, v, bg, cg, xi = project(xp, sa, ca, g_attn_pre[l], w_in[l])
        ao = diff_attention(q, k, v, lam, g_subln[l], lambda_init)
        co = gated_short_conv(bg, cg, xi, conv_w[l], conv_b[l])
        xp = finish_layer(xp, ao, co, ga, sf, cf, gf, w_out[l], g_attn_post[l],
                          g_ffn_pre[l], g_ffn_post[l], w_ffn_in[l], w_ffn_out[l])
        new_k.append(k.reshape(bp, sp, N_HEADS, 2 * HEAD_DIM))
        new_v.append(v)

        sa, ca, ga, sf, cf, gf = [m[:, None, :] for m in modulation(c, w_ada[l], b_ada[l])]
        q, k, v, bg, cg, xi = project(xs, sa, ca, g_attn_pre[l], w_in[l])
        q = apply_axial_rotary(q, tables)
        k = apply_axial_rotary(k, tables)
        kc = cache_k[:, l].reshape(bs, cache_k.shape[2], N_HEADS, 2, HEAD_DIM)
        k_all = jnp.concatenate([kc, k], axis=1)
        v_all = jnp.concatenate([cache_v[:, l], v], axis=1)
        ao = diff_attention(q, k_all, v_all, lam, g_subln[l], lambda_init)
        co = gated_short_conv(bg, cg, xi, conv_w[l], conv_b[l])
        xs = finish_layer(xs, ao, co, ga, sf, cf, gf, w_out[l], g_attn_post[l],
                          g_ffn_pre[l], g_ffn_post[l], w_ffn_in[l], w_ffn_out[l])

    state_k = jnp.stack(new_k, axis=1)
    state_v = jnp.stack(new_v, axis=1)
    return (xp, xs, state_k, state_v)
```

```python
import numpy as np
import ml_dtypes
from contextlib import ExitStack
import concourse.bass as bass
import concourse.mybir as mybir
from concourse.bass_utils import run_bass_kernel_spmd

F32 = mybir.dt.float32
BF16 = mybir.dt.bfloat16
AF = mybir.ActivationFunctionType
ALU = mybir.AluOpType

D = 1024
NS = 4096
NP = 512
NT = NS + NP
DFF = 2816
NJ = 22
EPS = 1e-6
ATTN_SCALE = 0.125
LAMBDA_INIT = 0.2
NKC = 34


class Sem:
    def __init__(self, nc, name):
        self.h = nc.alloc_semaphore(name)
        self.v = 0


class Res:
    __slots__ = ("w", "r", "excl")

    def __init__(self, excl=False):
        self.w = None
        self.r = {}
        self.excl = excl


def eval_name(loc, name):
    return loc[name]


class Rec:
    def __init__(self):
        self.call = None

    def __getattr__(self, name):
        def f(*a, **k):
            self.call = (name, a, k)
            return self
        return f


def _replay(e, call):
    name, a, k = call
    return getattr(e, name)(*a, **k)


class Queue:
    def __init__(self, nc, name):
        self.name = name
        self.sem = Sem(nc, "q_" + name)
        self.waited = {}
        self.ops = []


class Prog:
    def __init__(self, nc):
        self.nc = nc
        self.pe = Queue(nc, "pe")
        self.act = Queue(nc, "act")
        self.dve = Queue(nc, "dve")
        self.pool = Queue(nc, "pool")
        self.sp = Queue(nc, "sp")
        self.dma_sems = []

    def new_dma_sem(self, name):
        s = Sem(self.nc, name)
        self.dma_sems.append(s)
        return s

    def issue(self, q, fn, reads=(), writes=(), inc=True, dma_sem=None):
        ex = [r for r in reads if r.excl and r not in writes]
        if ex:
            reads = [r for r in reads if not r.excl]
            writes = list(writes) + ex
        deps = {}
        for r in reads:
            if r.w is not None:
                s, v = r.w
                if deps.get(s, 0) < v:
                    deps[s] = v
        for w in writes:
            if w.w is not None:
                s, v = w.w
                if deps.get(s, 0) < v:
                    deps[s] = v
            for s, v in w.r.items():
                if deps.get(s, 0) < v:
                    deps[s] = v
        for s, v in deps.items():
            if s is q.sem and q is self.pe:
                continue
            if q.waited.get(s, 0) < v:
                q.waited[s] = v
                q.ops.append(lambda e, s=s, v=v: e.wait_ge(s.h, v))
        rec = Rec()
        fn(rec)
        call = rec.call
        assert call is not None
        if dma_sem is not None:
            dma_sem.v += 16
            stamp = (dma_sem, dma_sem.v)
            q.ops.append(lambda e, call=call, s=dma_sem: _replay(e, call).then_inc(s.h, 16))
        elif inc:
            q.sem.v += 1
            stamp = (q.sem, q.sem.v)
            q.ops.append(lambda e, call=call, s=q.sem: _replay(e, call).then_inc(s.h, 1))
        else:
            stamp = (q.sem, q.sem.v + 1)
            q.ops.append(lambda e, call=call: _replay(e, call))
        for r in reads:
            if r.r.get(stamp[0], 0) < stamp[1]:
                r.r[stamp[0]] = stamp[1]
        for w in writes:
            w.w = stamp
            w.r = {}

    def barrier(self):
        qs = [self.pe, self.act, self.dve, self.pool, self.sp]
        sems = [q.sem for q in qs] + [s for s in self.dma_sems if not getattr(s, "nobarrier", False)]
        for q in qs:
            for s in sems:
                if s is q.sem:
                    continue
                if s.v > 0 and q.waited.get(s, 0) < s.v:
                    q.waited[s] = s.v
                    q.ops.append(lambda e, s=s, v=s.v: e.wait_ge(s.h, v))


def build_nc(stage=99, dbg=None):
    nc = bass.Bass("TRN2", target_bir_lowering=False)

    def din(name, shape, dt=F32):
        return nc.dram_tensor(name, list(shape), dt, kind="ExternalInput").ap()

    def dout(name, shape, dt=F32):
        return nc.dram_tensor(name, list(shape), dt, kind="ExternalOutput").ap()

    xs_d = din("xs", [NS, D])
    xp_d = din("xp", [NP, D])
    ck_d = din("ck", [256, 512])
    cv_d = din("cv", [256, 512])
    condT_d = din("condT", [128, 8, 2])
    wada_d = din("wada", [D, 6 * D])
    badaT_d = din("badaT", [128, 48])
    gvec_d = din("gvec", [128, 4, 8])
    wA_d = [din("wA0", [D, 2816]), din("wA1", [D, 1280])]
    cw_d = din("cw", [128, 4, 4])
    lamv_d = din("lamv", [128, 4, 64])
    gsub_d = din("gsub", [128, 1])
    wout_d = din("wout", [D, D])
    F1_d = din("F1", [NJ, 128, 8 * 256])
    F2_d = din("F2", [8, 128, NJ * 128])
    rot_d = din("rot", [8, 128, 2, 512])
    identb_d = din("identb", [128, 128], BF16)
    identf_d = din("identf", [128, 128])

    ys_d = dout("ys", [NS, D])
    yp_d = dout("yp", [NP, D])
    sk_d = dout("sk", [NP, 512])
    sv_d = dout("sv", [NP, 512])

    S1_d = nc.dram_tensor("S1", [NJ, 128, 8 * 256], BF16).ap()
    S2_d = nc.dram_tensor("S2", [8, 128, NJ * 128], BF16).ap()

    P = Prog(nc)
    pe, act, dve, pool, sp = P.pe, P.act, P.dve, P.pool, P.sp
    I = P.issue

    with ExitStack() as es:
        TOTAL = 207 * 1024
        M = es.enter_context(nc.sbuf_tensor("M", [128, TOTAL // 2], BF16))
        ps = es.enter_context(nc.psum_tensor("ps", [128, 8, 512], F32))

        def carve(off, shape, dt):
            n = int(np.prod(shape))
            nbytes = n * (4 if dt == F32 else 2)
            assert off % 4 == 0 and off + nbytes <= TOTAL, (off, nbytes)
            v = M[:, off // 2: (off + nbytes) // 2]
            if dt == F32:
                v = v.bitcast(F32)
            if len(shape) == 2:
                v = v.rearrange("p (a b) -> p a b", b=shape[1])
            elif len(shape) == 3:
                v = v.rearrange("p (a b c) -> p a b c", b=shape[1], c=shape[2])
            return v

        class Alloc:
            def __init__(self, base, limit):
                self.o = base
                self.limit = limit

            def __call__(self, shape, dt):
                n = int(np.prod(shape)) * (4 if dt == F32 else 2)
                n = (n + 63) // 64 * 64
                off = self.o
                self.o += n
                assert self.o <= self.limit, (self.o, self.limit)
                return carve(off, shape, dt)

        KB = 1024
        ca = Alloc(0, 6 * KB)
        identb = ca([128], BF16)
        identf = ca([128], F32)
        modv = ca([2, 6, 8], F32)
        cwt = ca([4, 4], F32)
        neglam = ca([1], F32)
        gs = ca([1], F32)
        stat = ca([64], F32)
        scT = ca([8, 2], F32)
        gvec = ca([4, 8], F32)
        badaT = ca([48], F32)
        epsc = ca([1], F32)
        zprev = ca([4], F32)
        coT = carve(6 * KB, [4, NT], BF16)
        aoT = carve(42 * KB, [4, NT], BF16)
        qa = Alloc(78 * KB, 135 * KB)
        QT = qa([2, NS], BF16)
        KT = qa([2, 256 + NS], BF16)
        VA = qa([NKC, 2, 130], BF16)
        QTp = qa([2, NP], BF16)
        KTp = qa([2, NP], BF16)
        VAp = qa([4, 2, 130], BF16)
        ZB = 135 * KB

        z0 = Alloc(6 * KB, 60 * KB)
        wa = [z0([8, 512], F32), z0([8, 512], F32)]
        condT = z0([8, 2], F32)
        lamv = z0([4, 64], F32)
        lj = z0([64], F32)
        diag = [z0([128], F32), z0([128], F32)]
        gsubt = z0([1], F32)

        Wp = carve(ZB, [8, 2816], BF16)
        WpB = carve(ZB, [8, 1280], BF16)
        w1 = Alloc(ZB + 44 * KB, TOTAL)
        xt = [w1([D], F32), w1([D], F32)]
        uT = [w1([8, 512], BF16), w1([8, 512], BF16)]
        xn = w1([D], BF16)
        xn_2 = w1([D], BF16)
        w1b = Alloc(60 * KB, 78 * KB)
        w1c = Alloc(ZB + 20 * KB, ZB + 44 * KB)
        def p1work(al):
            d = {}
            d["rot"] = al([2, 512], F32)
            _t2 = al([512], F32)
            d["t2"] = [_t2, _t2]
            d["zb"] = al([514], F32)
            d["yb"] = al([512], F32)
            d["cgs"] = _t2
            d["xt2"] = al([D], F32)
            d["kst"] = al([256], F32)
            d["vst"] = al([256], F32)
            d["ckb"] = al([2, 256], BF16)
            if al is w1c:
                d["wac"] = al([8, 128], F32)
            d["hcg"] = al([4], F32)
            d["zh"] = al([4], F32)
            return d
        p1wA = p1work(w1b)
        p1wB = p1work(w1c)

        z2 = Alloc(ZB + 44 * KB, TOTAL)
        PT = [z2([2, 512], BF16) for _ in range(3)]
        AO = z2([4, 128], F32)
        otmp = z2([128], F32)
        aon = z2([4, 128], BF16)
        rz = z2([8], F32)
        junk = z2([128], F32)

        z3a = Alloc(78 * KB, ZB)
        x1bb = [z3a([4, D], F32), z3a([4, D], F32)]
        hT = z3a([NJ, 512], BF16)
        sg = [z3a([512], BF16), z3a([512], BF16)]
        woutb = carve(ZB, [8, D], BF16)
        z3 = Alloc(ZB + 16 * KB, TOTAL)
        GA = z3([D], F32)
        GF = z3([D], F32)
        diag3 = [z3([128], F32), z3([128], F32)]
        ring = [z3([2816], BF16) for _ in range(3)]
        u2T = z3([8, 512], BF16)
        fTall = z3([8, 512], F32)
        xn2 = z3([D], BF16)
        xn2b = z3([D], BF16)
        sgj = z3([2, 512], BF16)
        r_sgj = Res()

        bank = [Res(excl=True) for _ in range(8)]

        def psb(b, n=1):
            return ps[:, b:b + n, :]

        def ps_bf(b):
            return ps[:, b, :].bitcast(BF16)

        stat_i = [0]

        def newstat(n=1):
            i = stat_i[0]
            if i + n > 64:
                i = 0
            stat_i[0] = i + n
            return stat[:, i:i + n], Res()

        r_Wp = Res()
        s_wp = P.new_dma_sem("d_wp")
        s_ckb = P.new_dma_sem("d_ckb")
        r_ckb_d = {}

        def load_weights(pa):
            if pa in r_ckb_d:
                return
            W = Wp if pa == 0 else WpB
            ncols = 2816 if pa == 0 else 1280
            ckb = (p1wA if pa == 0 else p1wB)["ckb"]
            wv = wA_d[pa].rearrange("(kc p) c -> p kc c", p=128)
            for kc in range(8):
                c0 = 0
                while c0 < ncols:
                    c1 = min(ncols, c0 + 1408)
                    I(pool, lambda e: e.dma_start(out=W[:, kc, c0:c1], in_=wv[:, kc, c0:c1]),
                      writes=[r_Wp], dma_sem=s_wp)
                    c0 = c1
            r_ckb_d[pa] = Res()
            ckv = ck_d.rearrange("(c p) f -> p c f", p=128)
            I(pool, lambda e: e.dma_start(out=ckb, in_=ckv[:, :, pa * 256:(pa + 1) * 256]), writes=[r_ckb_d[pa]],
              dma_sem=s_ckb)

        if stage >= 0.91:
            load_weights(0)
        r_small = Res()
        r_ident = r_small
        s_c = P.new_dma_sem("d_const")
        I(sp, lambda e: e.dma_start(out=identb, in_=identb_d[:, :]), writes=[r_small], dma_sem=s_c)
        I(sp, lambda e: e.dma_start(out=identf, in_=identf_d[:, :]), writes=[r_small], dma_sem=s_c)
        I(sp, lambda e: e.dma_start(out=cwt, in_=cw_d[:, :, :]), writes=[r_small], dma_sem=s_c)
        I(sp, lambda e: e.dma_start(out=gvec, in_=gvec_d[:, :, :]), writes=[r_small], dma_sem=s_c)
        I(sp, lambda e: e.dma_start(out=badaT, in_=badaT_d[:, :]), writes=[r_small], dma_sem=s_c)
        I(sp, lambda e: e.dma_start(out=condT, in_=condT_d[:, :, :]), writes=[r_small], dma_sem=s_c)
        I(sp, lambda e: e.dma_start(out=lamv, in_=lamv_d[:, :, :]), writes=[r_small], dma_sem=s_c)
        I(sp, lambda e: e.dma_start(out=gsubt, in_=gsub_d[:, :]), writes=[r_small], dma_sem=s_c)

        s_scr = P.new_dma_sem("d_scr")
        s_scr.nobarrier = True
        r_scr = Res()

        def issue_scratch():
            for j in range(NJ if stage >= 0.2 else 0):
                I(pool, lambda e, j=j: e.dma_start(out=S1_d[j], in_=F1_d[j]), writes=[r_scr], dma_sem=s_scr)
            for c in range(8 if stage >= 0.2 else 0):
                for hh in range(2):
                    I(pool, lambda e, c=c, hh=hh: e.dma_start(out=S2_d[c][:, hh * 1408:(hh + 1) * 1408],
                                                              in_=F2_d[c][:, hh * 1408:(hh + 1) * 1408]),
                      writes=[r_scr], dma_sem=s_scr)

        r_eps = Res()
        I(dve, lambda e: e.memset(epsc, EPS), writes=[r_eps])
        r_scT = Res()
        I(act, lambda e: e.activation(out=scT, in_=condT, func=AF.Silu), reads=[r_small], writes=[r_scT])
        s1, r_s1 = newstat(2)
        r_lj = Res()
        for li in range(2):
            I(dve, lambda e, li=li: e.tensor_tensor(out=lj, in0=lamv[:, 2 * li, :], in1=lamv[:, 2 * li + 1, :], op=ALU.mult),
              reads=[r_small], writes=[r_lj])
            I(dve, lambda e, li=li: e.reduce_sum(out=s1[:, li:li + 1], in_=lj, axis=mybir.AxisListType.X),
              reads=[r_lj], writes=[r_s1])
        e12, r_e12 = newstat(2)
        I(act, lambda e: e.activation(out=e12, in_=s1, func=AF.Exp), reads=[r_s1], writes=[r_e12])
        r_lam = Res()
        I(dve, lambda e: e.tensor_tensor(out=neglam, in0=e12[:, 1:2], in1=e12[:, 0:1], op=ALU.subtract),
          reads=[r_e12], writes=[r_lam])
        I(dve, lambda e: e.tensor_scalar(out=neglam, in0=neglam, scalar1=-LAMBDA_INIT, scalar2=None, op0=ALU.add),
          reads=[r_lam], writes=[r_lam])
        r_gs = Res()
        I(dve, lambda e: e.tensor_scalar(out=gs, in0=gsubt, scalar1=1.0 - LAMBDA_INIT, scalar2=None, op0=ALU.mult),
          reads=[r_small], writes=[r_gs])

        s_wa = [P.new_dma_sem("d_wa0"), P.new_dma_sem("d_wa1")]
        r_wa = [Res(), Res()]
        wada_v = wada_d.rearrange("(kc p) c -> p kc c", p=128)
        mps = ps[:, 0, 0:32].rearrange("p (a b) -> p a b", b=2)
        mps2 = ps[:, 6, 448:512].rearrange("p (a b) -> p a b", b=2)
        for cb in range(4 if stage >= 0.4 else 0):
            sl = cb % 2
            for kc2 in range(2):
                I(sp, lambda e, cb=cb, sl=sl, kc2=kc2: e.dma_start(out=wa[sl][:, kc2 * 4:(kc2 + 1) * 4, :],
                                                                   in_=wada_v[:, kc2 * 4:(kc2 + 1) * 4, cb * 512:(cb + 1) * 512]),
                  writes=[r_wa[sl]], dma_sem=s_wa[sl])
            for ch in range(4):
                cidx = cb * 4 + ch
                for kc in range(8):
                    I(pe, lambda e, sl=sl, ch=ch, kc=kc, cidx=cidx: e.matmul(
                        mps[:, cidx, :], wa[sl][:, kc, ch * 128:(ch + 1) * 128], scT[:, kc, :],
                        start=(kc == 0), stop=(kc == 7)),
                      reads=[r_wa[sl], r_scT], writes=[bank[0]], inc=(kc == 7))
        mall = ca([48, 2], F32)
        r_mall = Res()
        r_modv = Res()
        for cnd in range(2):
            I(dve, lambda e, cnd=cnd: e.tensor_tensor(out=mall[:, 0:16, cnd], in0=mps[:, :, cnd], in1=badaT[:, 0:16],
                                                      op=ALU.add),
              reads=[bank[0], r_small], writes=[r_mall])
        for cnd in range(2):
            I(dve, lambda e, cnd=cnd: e.scalar_tensor_tensor(
                out=modv[:, cnd, 0, :], in0=mall[:, 8:16, cnd], scalar=1.0, in1=gvec[:, 0, :], op0=ALU.add,
                op1=ALU.mult), reads=[r_mall, r_small], writes=[r_modv])
            I(dve, lambda e, cnd=cnd: e.tensor_copy(out=modv[:, cnd, 1, :], in_=mall[:, 0:8, cnd]),
              reads=[r_mall], writes=[r_modv])

        r_modv2 = Res()
        s_wac = P.new_dma_sem("d_wac")
        r_wac = Res()
        mod_i = [16]

        def mod_dma(wac):
            cidx = mod_i[0]
            if cidx >= 48 or stage < 0.4:
                return
            I(pool, lambda e: e.dma_start(out=wac, in_=wada_v[:, :, cidx * 128:(cidx + 1) * 128]), writes=[r_wac],
              dma_sem=s_wac)

        def mod_step(wac):
            cidx = mod_i[0]
            if cidx >= 48 or stage < 0.4:
                return
            for kc in range(8):
                I(pe, lambda e, kc=kc: e.matmul(mps2[:, cidx - 16, :], wac[:, kc, :], scT[:, kc, :],
                                                start=(kc == 0), stop=(kc == 7)),
                  reads=[r_wac, r_scT], writes=[bank[6]], inc=(kc == 7))
            mod_i[0] += 1
            mod_dma(wac)

        def mod_finish():
            if stage < 0.4:
                return
            for cnd in range(2):
                I(dve, lambda e, cnd=cnd: e.tensor_tensor(out=mall[:, 16:48, cnd], in0=mps2[:, :, cnd],
                                                          in1=badaT[:, 16:48], op=ALU.add),
                  reads=[bank[6], r_small], writes=[r_mall])
            for cnd in range(2):
                I(dve, lambda e, cnd=cnd: e.scalar_tensor_tensor(
                    out=modv[:, cnd, 3, :], in0=mall[:, 32:40, cnd], scalar=1.0, in1=gvec[:, 2, :], op0=ALU.add,
                    op1=ALU.mult), reads=[r_mall, r_small], writes=[r_modv2])
                I(dve, lambda e, cnd=cnd: e.tensor_copy(out=modv[:, cnd, 4, :], in_=mall[:, 24:32, cnd]),
                  reads=[r_mall], writes=[r_modv2])
                for (dst, gt_i, g_i) in ((2, 2, 1), (5, 5, 3)):
                    I(dve, lambda e, cnd=cnd, dst=dst, gt_i=gt_i, g_i=g_i: e.tensor_tensor(
                        out=modv[:, cnd, dst, :], in0=mall[:, gt_i * 8:(gt_i + 1) * 8, cnd], in1=gvec[:, g_i, :],
                        op=ALU.mult),
                      reads=[r_mall, r_small], writes=[r_modv2])
        P.barrier()

        r_xt = [Res(), Res()]
        s_xt = [P.new_dma_sem("d_xt0"), P.new_dma_sem("d_xt1")]
        r_xn = Res()
        tcount = [0]

        def rstd_from(src_ap, src_res, n_feat, junk_out, junk_res):
            ss, r_ss = newstat()
            if not isinstance(src_res, list):
                src_res = [src_res]
            I(act, lambda e: e.activation(out=junk_out, in_=src_ap, func=AF.Square, scale=float(n_feat) ** -0.5,
                                          accum_out=ss),
              reads=src_res, writes=[junk_res, r_ss])
            sd, r_sd = newstat()
            I(act, lambda e: e.activation(out=sd, in_=ss, func=AF.Sqrt, bias=epsc[:, 0:1]), reads=[r_ss, r_eps],
              writes=[r_sd])
            rs, r_rs = newstat()
            I(dve, lambda e: e.reciprocal(out=rs, in_=sd), reads=[r_sd], writes=[r_rs])
            return rs, r_rs

        ev_i = [0]
        ev_mode = ["alt"]

        def transposes_to(xn_ap, xn_res, tb, dst, dst_res, cnd, ai, bi, tok0):
            pv = ps_bf(tb).rearrange("p (a b) -> p a b", b=128)
            for kc in range(8):
                I(pe, lambda e, kc=kc: e.transpose(pv[:, kc, :], xn_ap[:, kc * 128:(kc + 1) * 128], identb),
                  reads=[xn_res, r_ident], writes=[bank[tb]], inc=(kc == 7))
            ev_i[0] += 1
            for kc in range(8):
                Aap = modv[:, cnd, ai, kc:kc + 1]
                Bap = modv[:, cnd, bi, kc:kc + 1]
                if ev_mode[0] != "act" and ev_i[0] % 2 == 0:
                    I(dve, lambda e, kc=kc, Aap=Aap, Bap=Bap: e.tensor_scalar(
                        out=dst[:, kc, tok0:tok0 + 128], in0=pv[:, kc, :], scalar1=Aap, scalar2=Bap,
                        op0=ALU.mult, op1=ALU.add), reads=[bank[tb], r_modv, r_modv2], writes=[dst_res])
                else:
                    I(act, lambda e, kc=kc, Aap=Aap, Bap=Bap: e.activation(
                        out=dst[:, kc, tok0:tok0 + 128], in_=pv[:, kc, :], func=AF.Identity, bias=Bap, scale=Aap),
                      reads=[bank[tb], r_modv, r_modv2], writes=[dst_res])

        xnb = [xn, xn_2]
        r_xnb = [Res(), Res()]

        xt3l = [None]
        r_xt3 = [r_xt[0], r_xt[1], Res()]
        s_xt3 = [s_xt[0], s_xt[1], P.new_dma_sem("d_xt2")]

        def xt_slot(n):
            return xt[n % 3] if n % 3 < 2 else xt3l[0]

        def prep_a0(src_ap, n):
            xs = n % 3
            I(sp, lambda e: e.dma_start(out=xt_slot(n), in_=src_ap), writes=[r_xt3[xs]], dma_sem=s_xt3[xs])

        def prep_a1(n):
            xs = n % 3
            sl = n % 2
            xv = xt_slot(n)
            rs, r_rs = rstd_from(xv, r_xt3[xs], D, xnb[sl], r_xnb[sl])
            I(dve, lambda e: e.tensor_scalar(out=xnb[sl], in0=xv, scalar1=rs, scalar2=None, op0=ALU.mult),
              reads=[r_xt3[xs], r_rs], writes=[r_xnb[sl]])
            return sl

        def prep_a(src_ap):
            n = tcount[0]
            tcount[0] += 1
            prep_a0(src_ap, n)
            return prep_a1(n)

        def prep_b(sl, cnd, dst, dst_res, tok0, tb):
            transposes_to(xnb[sl], r_xnb[sl], tb, dst, dst_res, cnd, 0, 1, tok0)

        def prep_tile(src_ap, cnd, dst, dst_res, tok0, tb):
            sl = prep_a(src_ap)
            prep_b(sl, cnd, dst, dst_res, tok0, tb)

        r_uT = [Res(), Res()]
        r_QT = {}
        r_KT = {}
        r_VA = {}
        r_co = {}
        r_ao = {}
        s_st = P.new_dma_sem("d_state")
        s_stk = P.new_dma_sem("d_statek")
        s_cva = [P.new_dma_sem("d_cva0"), P.new_dma_sem("d_cva1")]
        s_rot = P.new_dma_sem("d_rot")
        r_ones = Res()
        if stage >= 0.6:
            I(pool, lambda e: e.memset(VA[:, :, :, 128:130], 1.0), writes=[r_ones])
            I(pool, lambda e: e.memset(VAp[:, :, :, 128:130], 1.0), writes=[r_ones])

        def blk_src(b, t):
            if b < 8:
                return xs_d[b * 512 + t * 128: b * 512 + (t + 1) * 128, :]
            return xp_d[t * 128:(t + 1) * 128, :]

        def p1_pass(pa):
            ev_mode[0] = "act" if pa == 0 else "alt"
            W = Wp if pa == 0 else WpB
            ncols = 2816 if pa == 0 else 1280
            wk = p1wA if pa == 0 else p1wB
            rot, t2, zb, yb, cgs, kst, vst = wk["rot"], wk["t2"], wk["zb"], wk["yb"], wk["cgs"], wk["kst"], wk["vst"]
            r_rot, r_zb, r_yb, r_kst, r_vst = Res(), Res(), Res(), Res(), Res()
            _r = Res()
            r_t2 = [_r, _r]
            r_cgs = _r
            xt3l[0] = wk["xt2"]
            ckb = wk["ckb"]
            load_weights(pa)
            r_ckb = r_ckb_d[pa]
            cvv = cv_d.rearrange("(c p) (h e) -> p c h e", p=128, e=128)
            for c in range(2):
                r_VA[c] = Res()
                I(pool, lambda e, c=c: e.dma_start(out=VA[:, c, :, 0:128], in_=cvv[:, c, 2 * pa:2 * pa + 2, :]),
                  writes=[r_VA[c]], dma_sem=s_cva[c])
            if pa == 0:
                issue_scratch()
            pv = ps_bf(1).rearrange("p (a b) -> p a b", b=128)
            if stage < 0.92:
                P.barrier()
                return
            for c in range(2):
                for hl in range(2):
                    I(pe, lambda e, c=c, hl=hl: e.transpose(pv[:, c * 2 + hl, :], ckb[:, c, hl * 128:(hl + 1) * 128],
                                                            identb),
                      reads=[r_ckb, r_ident], writes=[bank[1]], inc=(c == 1 and hl == 1))
            r_KT[("c", 0)] = Res()
            for hl in range(2):
                I(act, lambda e, hl=hl: e.activation(
                    out=KT[:, hl, 0:256].rearrange("p (c k) -> p c k", k=128), in_=pv[:, hl:4:2, :], func=AF.Copy),
                  reads=[bank[1]], writes=[r_KT[("c", 0)]])

            pair_ring = [(2, 3), (4, 5)]
            pr_i = [0]
            r_zp = Res()
            r_hz = Res()
            hcg, zh = wk["hcg"], wk["zh"]
            if stage < 0.93:
                P.barrier()
                return
            g0 = tcount[0]

            def gsrc(g):
                return blk_src(g // 4, g % 4)

            prep_a0(gsrc(0), g0)
            for t in range(4):
                prep_a0(gsrc(t + 1), g0 + t + 1)
                sl_ = prep_a1(g0 + t)
                prep_b(sl_, 0, uT[0], r_uT[0], t * 128, (0, 7)[t % 2])
            tcount[0] = g0 + 36
            for b in range(9):
                cnd = 0 if b < 8 else 1
                sl = b % 2
                u = uT[sl]
                pending = []
                if b + 1 < 9:
                    nsl = (b + 1) % 2
                    ncnd = 0 if b + 1 < 8 else 1
                    st_ = {}

                    def mk(t, nsl=nsl, ncnd=ncnd, b=b, st_=st_):
                        def part():
                            g = 4 * (b + 1) + t
                            if t <= 3:
                                if g + 1 < 36:
                                    prep_a0(gsrc(g + 1), g0 + g + 1)
                                st_[t] = prep_a1(g0 + g)
                            if t >= 1:
                                prep_b(st_[t - 1], ncnd, uT[nsl], r_uT[nsl], (t - 1) * 128, (0, 7)[(t - 1) % 2])
                        return part
                    pending = [mk(t) for t in range(5)]
                    pending.pop(0)()
                if pa == 1 and b == 0:
                    mod_dma(wk["wac"])
                if b < 8:
                    I(sp, lambda e, b=b: e.dma_start(out=rot, in_=rot_d[b]), writes=[r_rot], dma_sem=s_rot)
                for c in range(4 if stage >= 0.94 else 0):
                    hl = c % 2
                    isk = c >= 2
                    b0, b1 = pair_ring[pr_i[0] % 2]
                    pr_i[0] += 1
                    col = c * 256
                    for kc in range(8):
                        I(pe, lambda e, kc=kc, col=col, b0=b0: e.matmul(ps[:, b0, :], W[:, kc, col:col + 128], u[:, kc, :],
                                                                       start=(kc == 0), stop=(kc == 7)),
                          reads=[r_Wp, r_uT[sl]], writes=[bank[b0]], inc=(kc == 7))
                    if b < 8:
                        for kc in range(8):
                            I(pe, lambda e, kc=kc, col=col, b1=b1: e.matmul(ps[:, b1, :], W[:, kc, col + 128:col + 256],
                                                                           u[:, kc, :], start=(kc == 0), stop=(kc == 7)),
                              reads=[r_Wp, r_uT[sl]], writes=[bank[b1]], inc=(kc == 7))
                        if isk:
                            dst = KT[:, hl, 256 + b * 512: 256 + (b + 1) * 512]
                            rr = r_KT[(hl, b)] = Res()
                        else:
                            dst = QT[:, hl, b * 512:(b + 1) * 512]
                            rr = r_QT[(hl, b)] = Res()
                        ti = pr_i[0] % 2
                        I(dve, lambda e, b0=b0: e.tensor_tensor(out=ps[:, b0, :], in0=ps[:, b0, :], in1=rot[:, 0, :],
                                                                op=ALU.mult), reads=[bank[b0], r_rot], writes=[bank[b0]])
                        I(dve, lambda e, b1=b1, ti=ti: e.tensor_tensor(out=t2[ti], in0=ps[:, b1, :], in1=rot[:, 1, :],
                                                                       op=ALU.mult),
                          reads=[bank[b1], r_rot], writes=[r_t2[ti]])
                        I(dve, lambda e, b0=b0, ti=ti, dst=dst: e.tensor_tensor(out=dst, in0=ps[:, b0, :], in1=t2[ti],
                                                                                op=ALU.add),
                          reads=[bank[b0], r_t2[ti]], writes=[rr])
                    else:
                        if isk:
                            dst = KTp[:, hl, :]
                            rr = r_KT[("p", hl)] = Res()
                        else:
                            dst = QTp[:, hl, :]
                            rr = r_QT[("p", hl)] = Res()
                        I(act, lambda e, b0=b0, dst=dst: e.activation(out=dst, in_=ps[:, b0, :], func=AF.Copy),
                          reads=[bank[b0]], writes=[rr])
                    if pending:
                        pending.pop(0)()
                    if pa == 1:
                        mod_step(wk["wac"])
                while pending:
                    pending.pop(0)()
                for t in range(4 if stage >= 0.95 else 0):
                    vb = (1, 6)[t % 2] if b < 8 else 1
                    vps = ps[:, vb, 0:256]
                    for kc in range(8):
                        I(pe, lambda e, kc=kc, t=t: e.matmul(vps, u[:, kc, t * 128:(t + 1) * 128], W[:, kc, 1024:1280],
                                                             start=(kc == 0), stop=(kc == 7)),
                          reads=[r_Wp, r_uT[sl]], writes=[bank[vb]], inc=(kc == 7))
                    vin = vps.rearrange("p (h e) -> p h e", e=128)
                    if b < 8:
                        ch = 2 + b * 4 + t
                        r_VA[ch] = Res()
                        I(act, lambda e, ch=ch, vin=vin: e.activation(out=VA[:, ch, :, 0:128], in_=vin, func=AF.Copy),
                          reads=[bank[vb], r_ones], writes=[r_VA[ch]])
                    else:
                        r_VA[("p", t)] = Res()
                        I(act, lambda e, t=t, vin=vin: e.activation(out=VAp[:, t, :, 0:128], in_=vin, func=AF.Copy),
                          reads=[bank[vb], r_ones], writes=[r_VA[("p", t)]])
                        if stage >= 0.952:
                            I(dve, lambda e: e.tensor_copy(out=vst, in_=vps), reads=[bank[vb]], writes=[r_vst])
                            I(sp, lambda e, t=t: e.dma_start(out=sv_d[t * 128:(t + 1) * 128, pa * 256:(pa + 1) * 256],
                                                             in_=vst), reads=[r_vst], dma_sem=s_st)
                        if stage < 0.954:
                            continue
                        kb = 6
                        for hl in range(2):
                            for kc in range(8):
                                I(pe, lambda e, kc=kc, t=t, hl=hl: e.matmul(
                                    ps[:, kb, hl * 128:(hl + 1) * 128], u[:, kc, t * 128:(t + 1) * 128],
                                    W[:, kc, 512 + hl * 256: 512 + hl * 256 + 128], start=(kc == 0), stop=(kc == 7)),
                                  reads=[r_Wp, r_uT[sl]], writes=[bank[kb]], inc=(kc == 7))
                        I(act, lambda e: e.activation(out=kst, in_=ps[:, kb, 0:256], func=AF.Copy), reads=[bank[kb]],
                          writes=[r_kst])
                        I(sp, lambda e, t=t: e.dma_start(out=sk_d[t * 128:(t + 1) * 128, pa * 256:(pa + 1) * 256],
                                                         in_=kst), reads=[r_kst], dma_sem=s_stk)
                if pa == 0 and stage >= 0.96:
                    if b < 7:
                        un = uT[(b + 1) % 2]
                        hzp = ps[:, 1, 384:400].rearrange("p (j g t) -> p j g t", g=2, t=2)
                        for j in range(4):
                            for gi in (1, 2):
                                col = 1280 + gi * 512 + j * 128
                                for kc in range(8):
                                    I(pe, lambda e, kc=kc, col=col, j=j, gi=gi: e.matmul(
                                        hzp[:, j, gi - 1, :], W[:, kc, col:col + 128], un[:, kc, 0:2],
                                        start=(kc == 0), stop=(kc == 7)),
                                      reads=[r_Wp, r_uT[(b + 1) % 2]], writes=[bank[1]],
                                      inc=(kc == 7 and j == 3 and gi == 2))
                        I(dve, lambda e: e.tensor_copy(out=hcg, in_=hzp[:, :, 0, 0]), reads=[bank[1]], writes=[r_hz])
                        I(dve, lambda e: e.tensor_tensor(out=zh, in0=hzp[:, :, 1, 0], in1=hcg, op=ALU.mult),
                          reads=[bank[1], r_hz], writes=[r_hz])
                    for j in range(4):
                        bb = (2, 3, 4) if j % 2 == 0 else (5, 6, 7)
                        for gi in range(3):
                            col = 1280 + gi * 512 + j * 128
                            for kc in range(8):
                                I(pe, lambda e, kc=kc, col=col, bk=bb[gi]: e.matmul(
                                    ps[:, bk, :], W[:, kc, col:col + 128], u[:, kc, :], start=(kc == 0), stop=(kc == 7)),
                                  reads=[r_Wp, r_uT[sl]], writes=[bank[bb[gi]]], inc=(kc == 7))
                        segs = [(0, 512)] if b < 8 else [(0, 256), (256, 512)]
                        I(act, lambda e, bk=bb[1]: e.activation(out=cgs, in_=ps[:, bk, :], func=AF.Copy),
                          reads=[bank[bb[1]]], writes=[r_cgs])
                        if b >= 1 and b < 8:
                            I(dve, lambda e, j=j: e.tensor_copy(out=zb[:, 0:1], in_=zprev[:, j:j + 1]), reads=[r_zp],
                              writes=[r_zb])
                        else:
                            I(dve, lambda e: e.memset(zb[:, 0:1], 0.0), writes=[r_zb])
                        I(dve, lambda e, bk=bb[2]: e.tensor_tensor(out=zb[:, 1:513], in0=ps[:, bk, :], in1=cgs,
                                                                   op=ALU.mult),
                          reads=[bank[bb[2]], r_cgs], writes=[r_zb])
                        if b < 7:
                            I(dve, lambda e, j=j: e.tensor_copy(out=zb[:, 513:514], in_=zh[:, j:j + 1]), reads=[r_hz],
                              writes=[r_zb])
                        else:
                            I(dve, lambda e: e.memset(zb[:, 513:514], 0.0), writes=[r_zb])
                        I(dve, lambda e, j=j: e.tensor_copy(out=zprev[:, j:j + 1], in_=zb[:, 512:513]), reads=[r_zb],
                          writes=[r_zp])
                        for (a0, a1) in segs:
                            n = a1 - a0
                            if len(segs) == 2:
                                pass
                            I(dve, lambda e, j=j, a0=a0, a1=a1: e.tensor_scalar(
                                out=yb[:, a0:a1], in0=zb[:, 1 + a0:1 + a1], scalar1=cwt[:, j, 1:2], scalar2=cwt[:, j, 3:4],
                                op0=ALU.mult, op1=ALU.add), reads=[r_zb, r_small], writes=[r_yb])
                            lo = a0 + (1 if (len(segs) == 2 or b == 0 or b == 8) and True else 0)
                            la = a0 + 1 if (len(segs) == 2) else a0
                            I(dve, lambda e, j=j, la=la, a1=a1: e.scalar_tensor_tensor(
                                out=yb[:, la:a1], in0=zb[:, la:a1], scalar=cwt[:, j, 0:1], in1=yb[:, la:a1],
                                op0=ALU.mult, op1=ALU.add), reads=[r_zb, r_yb, r_small], writes=[r_yb])
                            ra = a1 - 1 if (len(segs) == 2) else a1
                            I(dve, lambda e, j=j, a0=a0, ra=ra: e.scalar_tensor_tensor(
                                out=yb[:, a0:ra], in0=zb[:, a0 + 2:ra + 2], scalar=cwt[:, j, 2:3], in1=yb[:, a0:ra],
                                op0=ALU.mult, op1=ALU.add), reads=[r_zb, r_yb, r_small], writes=[r_yb])
                        tokb = b * 512
                        r_co[(j, b)] = Res()
                        I(dve, lambda e, j=j, tokb=tokb, bk=bb[0]: e.tensor_tensor(
                            out=coT[:, j, tokb:tokb + 512], in0=ps[:, bk, :], in1=yb, op=ALU.mult),
                          reads=[bank[bb[0]], r_yb], writes=[r_co[(j, b)]])
            if pa == 1:
                mod_finish()
            P.barrier()

        accS = z2([9 * 129], F32)
        r_accS = Res()

        class AttnUnit:
            def __init__(self, hl, Qv, Kv, Vv, nq, nkc, dst, dst_res, rq, rk, rv):
                self.__dict__.update(locals())
                self.nqs = nq // 128
                self.r_PT = r_PT_glob
                self.sb = [(0, 1), (2, 3)]

            def acc(self, qs, m):
                idx = qs * 2 + m
                return 4 + idx // 3, (idx % 3) * 129

            def qk(self, kc):
                b0, b1 = self.sb[kc % 2]
                Kv, Qv, nq = self.Kv, self.Qv, self.nq
                I(pe, lambda e: e.matmul(ps[:, b0, 0:nq], Kv[0:64, kc * 128:(kc + 1) * 128], Qv[0:64, :], start=True,
                                         stop=True), reads=self.rq + self.rk(kc), writes=[bank[b0]], inc=False)
                I(pe, lambda e: e.matmul(ps[:, b1, 0:nq], Kv[64:128, kc * 128:(kc + 1) * 128], Qv[64:128, :],
                                         start=True, stop=True), reads=self.rq + self.rk(kc), writes=[bank[b1]], inc=True)

            def ex(self, kc):
                b0, b1 = self.sb[kc % 2]
                pt = PT[kc % 3]
                nq = self.nq
                I(act, lambda e: e.activation(out=pt[:, :, 0:nq], in_=ps[:, b0:b0 + 2, 0:nq], func=AF.Exp,
                                              scale=ATTN_SCALE),
                  reads=[bank[b0], bank[b1]], writes=[self.r_PT[kc % 3]])

            def av(self, kc):
                pt = PT[kc % 3]
                nqs, nkc = self.nqs, self.nkc
                for qs in range(nqs):
                    for m in range(2):
                        bk, off = self.acc(qs, m)
                        first = (kc == 0 and off == 0)
                        last = (qs == nqs - 1 and m == 1)
                        I(pe, lambda e: e.matmul(
                            ps[:, bk, off:off + 129], pt[:, m, qs * 128:(qs + 1) * 128], self.Vv(kc)[:, 0:129],
                            start=first, stop=(kc == nkc - 1), skip_group_check=True),
                          reads=[self.r_PT[kc % 3]] + self.rv(kc), writes=[bank[bk]], inc=last)

            def head(self):
                self.qk(0)
                if self.nkc > 1:
                    self.qk(1)
                self.ex(0)

            def rest(self, hooks):
                nkc = self.nkc
                for kc in range(nkc):
                    if kc + 2 < nkc:
                        self.qk(kc + 2)
                    if kc + 1 < nkc:
                        self.ex(kc + 1)
                    self.av(kc)
                    for f in hooks.get(kc, []):
                        f()

            def evac_copy(self):
                nacc = self.nqs * 2
                nb = (nacc + 2) // 3
                I(dve, lambda e: e.tensor_copy(out=accS[:, 0:nb * 387].rearrange("p (a c) -> p a c", c=387),
                                               in_=ps[:, 4:4 + nb, 0:387]),
                  reads=[bank[4 + i] for i in range(nb)], writes=[r_accS])
                av_ = accS.rearrange("p (a c) -> p a c", c=129)
                r_rz = Res()
                I(dve, lambda e: e.reciprocal(out=rz[:, 0:nacc], in_=av_[:, 0:nacc, 128]), reads=[r_accS], writes=[r_rz])
                rzv = rz[:, 0:nacc].rearrange("p (q m) -> p q m", m=2)
                I(dve, lambda e: e.tensor_scalar(out=rzv[:, :, 1], in0=rzv[:, :, 1], scalar1=neglam[:, 0:1], scalar2=None,
                                                 op0=ALU.mult), reads=[r_rz, r_lam], writes=[r_rz])
                self.ss, self.r_ss = newstat(4)
                self.r_AO, r_otmp, r_junk = Res(), Res(), Res()
                for qs in range(self.nqs):
                    I(dve, lambda e, qs=qs: e.tensor_scalar(
                        out=otmp, in0=av_[:, 2 * qs, 0:128], scalar1=rz[:, 2 * qs:2 * qs + 1], scalar2=None,
                        op0=ALU.mult), reads=[r_accS, r_rz], writes=[r_otmp])
                    I(dve, lambda e, qs=qs: e.scalar_tensor_tensor(
                        out=AO[:, qs, :], in0=av_[:, 2 * qs + 1, 0:128], scalar=rz[:, 2 * qs + 1:2 * qs + 2], in1=otmp,
                        op0=ALU.mult, op1=ALU.add), reads=[r_accS, r_rz, r_otmp], writes=[self.r_AO])
                    I(dve, lambda e, qs=qs: e.tensor_tensor(out=junk, in0=AO[:, qs, :], in1=AO[:, qs, :], op=ALU.mult),
                      reads=[self.r_AO], writes=[r_junk])
                    I(dve, lambda e, qs=qs: e.reduce_sum(out=self.ss[:, qs:qs + 1], in_=junk, axis=mybir.AxisListType.X),
                      reads=[r_junk], writes=[self.r_ss])

            def evac_finish(self):
                nqs = self.nqs
                ss, r_ss = self.ss, self.r_ss
                ln_, r_ln = newstat(4)
                I(act, lambda e: e.activation(out=ln_[:, 0:nqs], in_=ss[:, 0:nqs], func=AF.Ln, bias=epsc[:, 0:1],
                                              scale=1.0 / 128), reads=[r_ss, r_eps], writes=[r_ln])
                rs_, r_rs = newstat(4)
                I(act, lambda e: e.activation(out=rs_[:, 0:nqs], in_=ln_[:, 0:nqs], func=AF.Exp, scale=-0.5),
                  reads=[r_ln], writes=[r_rs])
                r_aon = r_aon_glob
                for qs in range(nqs):
                    I(dve, lambda e, qs=qs: e.tensor_scalar(out=aon[:, qs, :], in0=AO[:, qs, :],
                                                            scalar1=rs_[:, qs:qs + 1], scalar2=None, op0=ALU.mult),
                      reads=[self.r_AO, r_rs], writes=[r_aon])
                pv = ps_bf(7).rearrange("p (a b) -> p a b", b=128)
                for qs in range(nqs):
                    I(pe, lambda e, qs=qs: e.transpose(pv[:, qs, :], aon[:, qs, :], identb), reads=[r_aon, r_ident],
                      writes=[bank[7]], inc=(qs == nqs - 1))
                dst = self.dst
                I(dve, lambda e: e.tensor_scalar(out=dst, in0=pv[:, 0:nqs, :], scalar1=gs[:, 0:1], scalar2=None,
                                                 op0=ALU.mult),
                  reads=[bank[7], r_gs], writes=[self.dst_res])

        r_PT_glob = [Res(), Res(), Res()]
        r_aon_glob = Res()

        r_GA = Res()
        r_GF = Res()
        r_diag = [Res(), Res()]
        onesf = ca([128], F32)
        r_onesf = Res()
        di_ = [0]

        def build_G(Gt, r_Gt, gi, cnd):
            if stage < 0.8:
                return
            for kc in range(8):
                dsl = di_[0] % 2
                di_[0] += 1
                I(dve, lambda e: e.tensor_scalar(
                    out=diag3[dsl], in0=identf, scalar1=modv[:, cnd, gi, kc:kc + 1], scalar2=None, op0=ALU.mult),
                  reads=[r_ident, r_modv, r_modv2], writes=[r_diag[dsl]])
                bk = kc // 4
                I(pe, lambda e: e.matmul(ps[:, bk, (kc % 4) * 128:(kc % 4 + 1) * 128], onesf, diag3[dsl],
                                         start=True, stop=True),
                  reads=[r_diag[dsl], r_onesf], writes=[bank[bk]], inc=True)
            I(act, lambda e: e.activation(out=Gt.rearrange("p (a b) -> p a b", b=512), in_=ps[:, 0:2, :], func=AF.Copy),
              reads=[bank[0], bank[1]], writes=[r_Gt])

        r_wout = Res()
        s_wo = P.new_dma_sem("d_wout")
        wov = wout_d.rearrange("(kc p) c -> p kc c", p=128)
        p3_pre = [False]

        def prefetch_p3():
            if p3_pre[0]:
                return
            p3_pre[0] = True
            for kc in range(8 if stage >= 0.9 else 0):
                I(pool, lambda e: e.dma_start(out=woutb[:, kc, :], in_=wov[:, kc, :]), writes=[r_wout], dma_sem=s_wo)
            I(dve, lambda e: e.memset(onesf, 1.0), writes=[r_onesf])
            build_G(GA, r_GA, 2, 0)
            build_G(GF, r_GF, 5, 0)

        def p2_pass(pa):
            if pa == 0 and stage >= 3:
                load_weights(1)
            if pa == 1 and stage >= 5:
                prefetch_p3()
            units = []
            for b in range(8):
                for hl in range(2):
                    h = 2 * pa + hl
                    rr = r_ao[(h, b)] = Res()
                    units.append(AttnUnit(
                        hl, QT[:, hl, b * 512:(b + 1) * 512], KT[:, hl, :],
                        lambda kc, hl=hl: VA[:, kc, hl, :], 512, NKC,
                        aoT[:, h, b * 512:(b + 1) * 512].rearrange("p (q t) -> p q t", t=128), rr,
                        [r_QT[(hl, b)]],
                        lambda kc, hl=hl: [r_KT[("c", 0)]] if kc < 2 else [r_KT[(hl, (kc - 2) // 4)]],
                        lambda kc: [r_VA[kc]]))
            for s in range(2):
                for hl in range(2):
                    h = 2 * pa + hl
                    rr = r_ao[(h, 8, s)] = Res()
                    units.append(AttnUnit(
                        hl, QTp[:, hl, s * 256:(s + 1) * 256], KTp[:, hl, s * 256:(s + 1) * 256],
                        lambda kc, hl=hl, s=s: VAp[:, 2 * s + kc, hl, :], 256, 2,
                        aoT[:, h, NS + s * 256: NS + (s + 1) * 256].rearrange("p (q t) -> p q t", t=128), rr,
                        [r_QT[("p", hl)]], lambda kc, hl=hl: [r_KT[("p", hl)]],
                        lambda kc, s=s: [r_VA[("p", 2 * s + kc)]]))
            units[0].head()
            prev = None
            for i, u in enumerate(units):
                hooks = {}
                if prev is not None:
                    hooks[min(6, u.nkc - 1)] = [prev.evac_finish]
                u.rest(hooks)
                if i + 1 < len(units):
                    units[i + 1].head()
                u.evac_copy()
                prev = u
            prev.evac_finish()
            P.barrier()

        if stage >= 0.91:
            p1_pass(0)
        if stage >= 2:
            p2_pass(0)
        if stage >= 3:
            p1_pass(1)
        if stage >= 4:
            p2_pass(1)

        ev_mode[0] = "alt"
        prefetch_p3()
        s_ring = [P.new_dma_sem("d_ring%d" % i) for i in range(3)]
        r_ring = [Res() for _ in range(3)]
        ring_i = [0]
        s_x3 = [P.new_dma_sem("d_x30"), P.new_dma_sem("d_x31")]
        s_x1 = [[P.new_dma_sem("d_x1%d_%d" % (p_, i)) for i in range(4)] for p_ in range(2)]
        r_x3 = [Res(), Res()]
        s_y = [[P.new_dma_sem("d_y%d_%d" % (p_, i)) for i in range(4)] for p_ in range(2)]
        r_y = [Res(), Res()]
        r_x1 = [[Res() for _ in range(4)] for _ in range(2)]
        r_u2 = Res()
        r_xn2 = Res()
        r_hT = [Res() for _ in range(NJ)]
        r_sg = [Res(), Res()]
        r_fT = [Res(), Res()]
        tT = [(0, 1), (2, 3)]
        fR = [(4, 5), (6, 7)]
        t_i = [0]
        f_i = [0]
        x3_i = [0]
        y_i = [0]

        def ring_load(src_ap, n):
            sl = ring_i[0] % 3
            ring_i[0] += 1
            I(sp, lambda e: e.dma_start(out=ring[sl][:, 0:n], in_=src_ap), reads=[r_scr], writes=[r_ring[sl]],
              dma_sem=s_ring[sl])
            return sl

        xn2s = [xn2, xn2b]
        r_xn2s = [Res(), Res()]
        r_fT = Res()
        nblk = 9 if stage >= 5 else 0

        def tok_of(b, t):
            return b * 512 + t * 128

        hst = {}

        def head_a(b, t):
            tok = tok_of(b, t)
            par = b % 2
            x1t = x1bb[par][:, t, :]
            rx1 = r_x1[par][t]
            I(sp, lambda e: e.dma_start(out=x1t, in_=blk_src(b, t)), writes=[rx1], dma_sem=s_x1[par][t])
            b0, b1 = tT[t_i[0] % 2]
            t_i[0] += 1
            for half in range(2):
                bk = (b0, b1)[half]
                for kc in range(8):
                    if kc < 4:
                        lhs = aoT[:, kc, tok:tok + 128]
                        rl = [r_ao[(kc, b)]] if b < 8 else [r_ao[(kc, 8, t // 2)]]
                    else:
                        lhs = coT[:, kc - 4, tok:tok + 128]
                        rl = [r_co[(kc - 4, b)]]
                    I(pe, lambda e: e.matmul(
                        ps[:, bk, :], lhs, woutb[:, kc, half * 512:(half + 1) * 512], start=(kc == 0), stop=(kc == 7)),
                      reads=rl + [r_wout], writes=[bank[bk]], inc=(kc == 7))
            mixp = ps[:, b0:b0 + 2, :]
            ss, r_ss = newstat()
            I(act, lambda e: e.activation(out=sgj, in_=mixp, func=AF.Square, scale=float(D) ** -0.5, accum_out=ss),
              reads=[bank[b0], bank[b1]], writes=[r_sgj, r_ss])
            hst[(b, t)] = dict(b0=b0, b1=b1, mixp=mixp, ss=ss, r_ss=r_ss, x1t=x1t, rx1=rx1)

        def rstd_tail(ss, r_ss):
            sd, r_sd = newstat()
            I(act, lambda e: e.activation(out=sd, in_=ss, func=AF.Sqrt, bias=epsc[:, 0:1]), reads=[r_ss, r_eps],
              writes=[r_sd])
            rs, r_rs = newstat()
            I(dve, lambda e: e.reciprocal(out=rs, in_=sd), reads=[r_sd], writes=[r_rs])
            return rs, r_rs

        def head_b(b, t):
            s = hst[(b, t)]
            rs1, r_rs1 = rstd_tail(s["ss"], s["r_ss"])
            mixp, b0, b1 = s["mixp"], s["b0"], s["b1"]
            I(dve, lambda e: e.scalar_tensor_tensor(
                out=mixp, in0=mixp, scalar=rs1, in1=GA.rearrange("p (a b) -> p a b", b=512),
                op0=ALU.mult, op1=ALU.mult), reads=[bank[b0], bank[b1], r_rs1, r_GA], writes=[bank[b0], bank[b1]])

        def head_c(b, t):
            s = hst[(b, t)]
            mixp, b0, b1, x1t, rx1 = s["mixp"], s["b0"], s["b1"], s["x1t"], s["rx1"]
            x1v = x1t.rearrange("p (a b) -> p a b", b=512)
            I(dve, lambda e: e.tensor_tensor(out=x1v, in0=mixp, in1=x1v, op=ALU.add),
              reads=[bank[b0], bank[b1], rx1], writes=[rx1])

        def head_d(b, t):
            s = hst[(b, t)]
            xs_ = t % 2
            ss, r_ss = newstat()
            I(act, lambda e: e.activation(out=xn2s[xs_], in_=s["x1t"], func=AF.Square, scale=float(D) ** -0.5,
                                          accum_out=ss), reads=[s["rx1"]], writes=[r_xn2s[xs_], r_ss])
            s["rs2"], s["r_rs2"] = rstd_tail(ss, r_ss)

        def head_e(b, t):
            s = hst[(b, t)]
            xs_ = t % 2
            I(act, lambda e: e.activation(out=xn2s[xs_], in_=s["x1t"], func=AF.Identity, scale=s["rs2"]),
              reads=[s["rx1"], s["r_rs2"]], writes=[r_xn2s[xs_]])
            head_T(b, t)

        def head_T(b, t):
            cnd = 0 if b < 8 else 1
            fb = fR[f_i[0] % 2][0]
            f_i[0] += 1
            transposes_to(xn2s[t % 2], r_xn2s[t % 2], fb, u2T, r_u2, cnd, 3, 4, t * 128)

        def ffn_in(b, hooks):
            slots = {}
            for j in range(2):
                slots[j] = ring_load(S1_d[j], 2048)
            for j in range(NJ):
                if j + 2 < NJ:
                    slots[j + 2] = ring_load(S1_d[j + 2], 2048)
                sl = slots[j]
                wv = ring[sl][:, 0:2048].rearrange("p (kc c) -> p kc c", c=256)
                bg_, bu_ = fR[f_i[0] % 2]
                f_i[0] += 1
                for kc in range(8):
                    I(pe, lambda e: e.matmul(ps[:, bg_, :], wv[:, kc, 0:128], u2T[:, kc, :],
                                             start=(kc == 0), stop=(kc == 7)),
                      reads=[r_ring[sl], r_u2], writes=[bank[bg_]], inc=(kc == 7))
                for kc in range(8):
                    I(pe, lambda e: e.matmul(ps[:, bu_, :], wv[:, kc, 128:256], u2T[:, kc, :],
                                             start=(kc == 0), stop=(kc == 7)),
                      reads=[r_ring[sl], r_u2], writes=[bank[bu_]], inc=(kc == 7))
                si = j % 2
                I(act, lambda e: e.activation(out=sg[si], in_=ps[:, bg_, :], func=AF.Silu),
                  reads=[bank[bg_]], writes=[r_sg[si]])
                I(dve, lambda e: e.tensor_tensor(out=hT[:, j, :], in0=ps[:, bu_, :], in1=sg[si], op=ALU.mult),
                  reads=[bank[bu_], r_sg[si]], writes=[r_hT[j]])
                for f in hooks.get(j, []):
                    f()

        def ffn_out(b, hooks):
            slots = {}
            for c in range(2):
                slots[c] = ring_load(S2_d[c], 2816)
            for c in range(8):
                if c + 2 < 8:
                    slots[c + 2] = ring_load(S2_d[c + 2], 2816)
                sl = slots[c]
                wv = ring[sl][:, 0:2816].rearrange("p (j n) -> p j n", n=128)
                fbk = fR[f_i[0] % 2][0]
                f_i[0] += 1
                for j in range(NJ):
                    I(pe, lambda e: e.matmul(ps[:, fbk, :], wv[:, j, :], hT[:, j, :],
                                             start=(j == 0), stop=(j == NJ - 1)),
                      reads=[r_ring[sl], r_hT[j]], writes=[bank[fbk]], inc=(j == NJ - 1))
                I(act, lambda e: e.activation(out=fTall[:, c, :], in_=ps[:, fbk, :], func=AF.Copy),
                  reads=[bank[fbk]], writes=[r_fT])
                for f in hooks.get(c, []):
                    f()

        tst = {}

        def tail_a(b, t):
            b0, b1 = tT[t_i[0] % 2]
            t_i[0] += 1
            for c in range(8):
                bk = b0 if c < 4 else b1
                I(pe, lambda e: e.transpose(ps[:, bk, (c % 4) * 128:(c % 4 + 1) * 128],
                                            fTall[:, c, t * 128:(t + 1) * 128], identf),
                  reads=[r_fT, r_ident], writes=[bank[bk]], inc=(c == 3 or c == 7))
            fp_ = ps[:, b0:b0 + 2, :]
            ss, r_ss = newstat()
            I(act, lambda e: e.activation(out=sgj, in_=fp_, func=AF.Square, scale=float(D) ** -0.5, accum_out=ss),
              reads=[bank[b0], bank[b1]], writes=[r_sgj, r_ss])
            tst[(b, t)] = dict(b0=b0, b1=b1, fp_=fp_, ss=ss, r_ss=r_ss)

        def tail_b(b, t):
            s = tst[(b, t)]
            rs3, r_rs3 = rstd_tail(s["ss"], s["r_ss"])
            fp_, b0, b1 = s["fp_"], s["b0"], s["b1"]
            I(dve, lambda e: e.scalar_tensor_tensor(
                out=fp_, in0=fp_, scalar=rs3, in1=GF.rearrange("p (a b) -> p a b", b=512),
                op0=ALU.mult, op1=ALU.mult), reads=[bank[b0], bank[b1], r_rs3, r_GF], writes=[bank[b0], bank[b1]])

        def tail_c(b, t):
            s = tst[(b, t)]
            fp_, b0, b1 = s["fp_"], s["b0"], s["b1"]
            par = b % 2
            x1t = x1bb[par][:, t, :]
            rx1 = r_x1[par][t]
            x1v = x1t.rearrange("p (a b) -> p a b", b=512)
            I(dve, lambda e: e.tensor_tensor(out=x1v, in0=fp_, in1=x1v, op=ALU.add),
              reads=[bank[b0], bank[b1], rx1], writes=[rx1])
            if b < 8:
                ydst = ys_d[b * 512 + t * 128: b * 512 + (t + 1) * 128, :]
            else:
                ydst = yp_d[t * 128:(t + 1) * 128, :]
            I(sp, lambda e: e.dma_start(out=ydst, in_=x1t), reads=[rx1], dma_sem=s_y[par][t])

        def add_hook(hk, j, f):
            hk.setdefault(j, []).append(f)

        if nblk:
            for t in range(4):
                head_a(0, t); head_b(0, t); head_c(0, t); head_d(0, t); head_e(0, t)
        for b in range(nblk):
            hk = {}
            n = b + 1
            if n == 8:
                build_G(GA, r_GA, 2, 1)
            for t in range(4):
                base = 5 * t
                if b >= 1:
                    add_hook(hk, base, lambda b=b, t=t: tail_a(b - 1, t))
                    add_hook(hk, base + 1, lambda b=b, t=t: tail_b(b - 1, t))
                    add_hook(hk, base + 2, lambda b=b, t=t: tail_c(b - 1, t))
                if n < nblk:
                    add_hook(hk, base + 2, lambda n=n, t=t: head_a(n, t))
                    add_hook(hk, base + 3, lambda n=n, t=t: head_b(n, t))
                    add_hook(hk, base + 4, lambda n=n, t=t: head_c(n, t))
                    add_hook(hk, base + 5, lambda n=n, t=t: head_d(n, t))
            ffn_in(b, hk)
            hk = {}
            if n < nblk:
                for t in range(4):
                    add_hook(hk, t, lambda n=n, t=t: head_e(n, t))
            ffn_out(b, hk)
        if nblk:
            build_G(GF, r_GF, 5, 1)
            for t in range(4):
                tail_a(nblk - 1, t); tail_b(nblk - 1, t); tail_c(nblk - 1, t)
        P.barrier()
        if dbg:
            loc = dict(locals())
            s_dbg = P.new_dma_sem("d_dbg")
            for k, (name, shape, dt) in enumerate(dbg):
                dd = nc.dram_tensor("dbg%d" % k, list(shape), dt, kind="ExternalOutput").ap()
                src_ap = eval_name(loc, name)
                I(sp, lambda e, dd=dd, src_ap=src_ap: e.dma_start(out=dd, in_=src_ap), dma_sem=s_dbg)
            P.barrier()

        with nc.Block() as block:
            @block.tensor
            def _(e):
                for f in pe.ops:
                    f(e)

            @block.scalar
            def _(e):
                for f in act.ops:
                    f(e)

            @block.vector
            def _(e):
                for f in dve.ops:
                    f(e)

            @block.gpsimd
            def _(e):
                for f in pool.ops:
                    f(e)

            @block.sync
            def _(e):
                for f in sp.ops:
                    f(e)
    return nc


_NC_CACHE = {}


def _perm_sw(cols):
    c = np.asarray(cols)
    g, r = c // 64, c % 64
    return g * 64 + ((r // 16) ^ 1) * 16 + (r % 16)


def _rot_tables():
    n_freq = 16
    inv = (1.0 / (np.float32(10000.0) ** (np.arange(n_freq, dtype=np.float32) / np.float32(n_freq)))).astype(np.float32)
    t = np.arange(NS)
    row = (t // 64).astype(np.float32)
    col = (t % 64).astype(np.float32)
    ar = (row[:, None] * inv[None, :]).astype(np.float32)
    ac = (col[:, None] * inv[None, :]).astype(np.float32)
    cr, sr, cc, sc = np.cos(ar), np.sin(ar), np.cos(ac), np.sin(ac)
    C = np.concatenate([cr, cr, cc, cc], axis=1)
    S = np.concatenate([-sr, sr, -sc, sc], axis=1)
    C = np.concatenate([C, C], axis=1).T
    S = np.concatenate([S, S], axis=1).T
    rot = np.stack([C.reshape(128, 8, 512), S.reshape(128, 8, 512)], axis=2)
    return np.ascontiguousarray(rot.transpose(1, 0, 2, 3)).astype(np.float32)


def kernel(x_prompt, x_sample, cache_k, cache_v, c, c_ctx, w_ada, b_ada,
           g_attn_pre, g_attn_post, g_ffn_pre, g_ffn_post, w_in, conv_w, conv_b,
           lambda_q1, lambda_k1, lambda_q2, lambda_k2, g_subln, w_out,
           w_ffn_in, w_ffn_out):
    f = lambda a: np.ascontiguousarray(np.asarray(a, dtype=np.float32))
    x_prompt, x_sample, cache_k, cache_v, c, c_ctx = map(f, (x_prompt, x_sample, cache_k, cache_v, c, c_ctx))
    w_ada, b_ada, w_in, w_out, w_ffn_in, w_ffn_out = map(f, (w_ada[0], b_ada[0], w_in[0], w_out[0], w_ffn_in[0],
                                                             w_ffn_out[0]))
    conv_w, conv_b = f(conv_w[0]), f(conv_b[0])
    def fm(v):
        return f(v).reshape(8, 128).T
    gvec = np.ascontiguousarray(np.stack([fm(g_attn_pre[0]), fm(g_attn_post[0]), fm(g_ffn_pre[0]), fm(g_ffn_post[0])],
                                         axis=1))
    badaT = np.ascontiguousarray(b_ada.reshape(48, 128).T)
    wA = []
    for pa in range(2):
        cols = []
        for sec in (0, 512):
            for hl in range(2):
                h = 2 * pa + hl
                base = sec + h * 128 + np.arange(128)
                cols.append(base)
                cols.append(sec + _perm_sw(h * 128 + np.arange(128)))
        cols.append(1024 + 2 * pa * 128 + np.arange(256))
        if pa == 0:
            cols.append(1536 + np.arange(1536))
        cols = np.concatenate(cols)
        wA.append(np.ascontiguousarray(w_in[:, cols]))
    cw = np.ascontiguousarray(np.stack([conv_w[0].reshape(4, 128).T, conv_w[1].reshape(4, 128).T,
                                        conv_w[2].reshape(4, 128).T, conv_b.reshape(4, 128).T], axis=2))
    lamv = np.ascontiguousarray(np.broadcast_to(
        np.stack([f(lambda_q1[0]), f(lambda_k1[0]), f(lambda_q2[0]), f(lambda_k2[0])])[None], (128, 4, 64)))
    gsub = f(g_subln[0]).reshape(128, 1)
    wi = w_ffn_in.reshape(8, 128, 2, NJ, 128)
    F1 = np.ascontiguousarray(wi.transpose(3, 1, 0, 2, 4)).reshape(NJ, 128, 8 * 256)
    wo = w_ffn_out.reshape(NJ, 128, 8, 128)
    F2 = np.ascontiguousarray(wo.transpose(2, 1, 0, 3)).reshape(8, 128, NJ * 128)
    rot = _rot_tables()
    identf = np.eye(128, dtype=np.float32)
    identb = identf.astype(ml_dtypes.bfloat16)

    n = 8
    in_maps = []
    for core in range(n):
        cond = np.stack([c[core], c_ctx])
        condT = np.ascontiguousarray(cond.reshape(2, 8, 128).transpose(2, 1, 0))
        in_maps.append({
            "xs": x_sample[core], "xp": np.ascontiguousarray(x_prompt[2 * core:2 * core + 2].reshape(NP, D)),
            "ck": np.ascontiguousarray(cache_k[core, 0].reshape(256, 512)),
            "cv": np.ascontiguousarray(cache_v[core, 0].reshape(256, 512)),
            "condT": condT, "wada": w_ada, "badaT": badaT, "gvec": gvec, "wA0": wA[0], "wA1": wA[1], "cw": cw,
            "lamv": lamv, "gsub": gsub, "wout": w_out, "F1": F1, "F2": F2, "rot": rot, "identb": identb,
            "identf": identf,
        })
    if "nc" not in _NC_CACHE:
        _NC_CACHE["nc"] = build_nc()
    nc = _NC_CACHE["nc"]
    res = run_bass_kernel_spmd(nc, in_maps, core_ids=list(range(n)))
    r = res.results
    y_sample = np.stack([r[i]["ys"] for i in range(n)]).astype(np.float32)
    y_prompt = np.concatenate([r[i]["yp"].reshape(2, 256, D) for i in range(n)]).astype(np.float32)
    state_k = np.concatenate([r[i]["sk"].reshape(2, 1, 256, 4, 128) for i in range(n)]).astype(np.float32)
    state_v = np.concatenate([r[i]["sv"].reshape(2, 1, 256, 4, 128) for i in range(n)]).astype(np.float32)
    return (y_prompt, y_sample, state_k, state_v)
```

```python
import numpy as np
import ml_dtypes
from contextlib import ExitStack
import concourse.bass as bass
import concourse.mybir as mybir
from concourse.bass_utils import run_bass_kernel_spmd

F32 = mybir.dt.float32
BF16 = mybir.dt.bfloat16
AF = mybir.ActivationFunctionType
ALU = mybir.AluOpType

D = 1024
NS = 4096
NP = 512
NT = NS + NP
DFF = 2816
NJ = 22
EPS = 1e-6
ATTN_SCALE = 0.125
LAMBDA_INIT = 0.2
NKC = 34


class Sem:
    def __init__(self, nc, name):
        self.h = nc.alloc_semaphore(name)
        self.v = 0


class Res:
    __slots__ = ("w", "r", "excl")

    def __init__(self, excl=False):
        self.w = None
        self.r = {}
        self.excl = excl


def eval_name(loc, name):
    return loc[name]


class Rec:
    def __init__(self):
        self.call = None

    def __getattr__(self, name):
        def f(*a, **k):
            self.call = (name, a, k)
            return self
        return f


def _replay(e, call):
    name, a, k = call
    return getattr(e, name)(*a, **k)


class Queue:
    def __init__(self, nc, name):
        self.name = name
        self.sem = Sem(nc, "q_" + name)
        self.waited = {}
        self.ops = []


class Prog:
    def __init__(self, nc):
        self.nc = nc
        self.pe = Queue(nc, "pe")
        self.act = Queue(nc, "act")
        self.dve = Queue(nc, "dve")
        self.pool = Queue(nc, "pool")
        self.sp = Queue(nc, "sp")
        self.dma_sems = []

    def new_dma_sem(self, name):
        s = Sem(self.nc, name)
        self.dma_sems.append(s)
        return s

    def issue(self, q, fn, reads=(), writes=(), inc=True, dma_sem=None):
        ex = [r for r in reads if r.excl and r not in writes]
        if ex:
            reads = [r for r in reads if not r.excl]
            writes = list(writes) + ex
        deps = {}
        for r in reads:
            if r.w is not None:
                s, v = r.w
                if deps.get(s, 0) < v:
                    deps[s] = v
        for w in writes:
            if w.w is not None:
                s, v = w.w
                if deps.get(s, 0) < v:
                    deps[s] = v
            for s, v in w.r.items():
                if deps.get(s, 0) < v:
                    deps[s] = v
        for s, v in deps.items():
            if s is q.sem and q is self.pe:
                continue
            if q.waited.get(s, 0) < v:
                q.waited[s] = v
                q.ops.append(lambda e, s=s, v=v: e.wait_ge(s.h, v))
        rec = Rec()
        fn(rec)
        call = rec.call
        assert call is not None
        if dma_sem is not None:
            dma_sem.v += 16
            stamp = (dma_sem, dma_sem.v)
            q.ops.append(lambda e, call=call, s=dma_sem: _replay(e, call).then_inc(s.h, 16))
        elif inc:
            q.sem.v += 1
            stamp = (q.sem, q.sem.v)
            q.ops.append(lambda e, call=call, s=q.sem: _replay(e, call).then_inc(s.h, 1))
        else:
            stamp = (q.sem, q.sem.v + 1)
            q.ops.append(lambda e, call=call: _replay(e, call))
        for r in reads:
            if r.r.get(stamp[0], 0) < stamp[1]:
                r.r[stamp[0]] = stamp[1]
        for w in writes:
            w.w = stamp
            w.r = {}

    def barrier(self):
        qs = [self.pe, self.act, self.dve, self.pool, self.sp]
        sems = [q.sem for q in qs] + [s for s in self.dma_sems if not getattr(s, "nobarrier", False)]
        for q in qs:
            for s in sems:
                if s is q.sem:
                    continue
                if s.v > 0 and q.waited.get(s, 0) < s.v:
                    q.waited[s] = s.v
                    q.ops.append(lambda e, s=s, v=s.v: e.wait_ge(s.h, v))


def build_nc(stage=99, dbg=None):
    nc = bass.Bass("TRN2", target_bir_lowering=False)

    def din(name, shape, dt=F32):
        return nc.dram_tensor(name, list(shape), dt, kind="ExternalInput").ap()

    def dout(name, shape, dt=F32):
        return nc.dram_tensor(name, list(shape), dt, kind="ExternalOutput").ap()

    xs_d = din("xs", [NS, D])
    xp_d = din("xp", [NP, D])
    ck_d = din("ck", [256, 512])
    cv_d = din("cv", [256, 512])
    condT_d = din("condT", [128, 8, 2])
    wada_d = din("wada", [D, 6 * D])
    badaT_d = din("badaT", [128, 48])
    gvec_d = din("gvec", [128, 4, 8])
    wA_d = [din("wA0", [D, 2816]), din("wA1", [D, 1280])]
    cw_d = din("cw", [128, 4, 4])
    lamv_d = din("lamv", [128, 4, 64])
    gsub_d = din("gsub", [128, 1])
    wout_d = din("wout", [D, D])
    F1_d = din("F1", [NJ, 128, 8 * 256])
    F2_d = din("F2", [8, 128, NJ * 128])
    rot_d = din("rot", [8, 128, 2, 512])
    identb_d = din("identb", [128, 128], BF16)
    identf_d = din("identf", [128, 128])

    ys_d = dout("ys", [NS, D])
    yp_d = dout("yp", [NP, D])
    sk_d = dout("sk", [NP, 512])
    sv_d = dout("sv", [NP, 512])

    S1_d = nc.dram_tensor("S1", [NJ, 128, 8 * 256], BF16).ap()
    S2_d = nc.dram_tensor("S2", [8, 128, NJ * 128], BF16).ap()

    P = Prog(nc)
    pe, act, dve, pool, sp = P.pe, P.act, P.dve, P.pool, P.sp
    I = P.issue

    with ExitStack() as es:
        TOTAL = 207 * 1024
        M = es.enter_context(nc.sbuf_tensor("M", [128, TOTAL // 2], BF16))
        ps = es.enter_context(nc.psum_tensor("ps", [128, 8, 512], F32))

        def carve(off, shape, dt):
            n = int(np.prod(shape))
            nbytes = n * (4 if dt == F32 else 2)
            assert off % 4 == 0 and off + nbytes <= TOTAL, (off, nbytes)
            v = M[:, off // 2: (off + nbytes) // 2]
            if dt == F32:
                v = v.bitcast(F32)
            if len(shape) == 2:
                v = v.rearrange("p (a b) -> p a b", b=shape[1])
            elif len(shape) == 3:
                v = v.rearrange("p (a b c) -> p a b c", b=shape[1], c=shape[2])
            return v

        class Alloc:
            def __init__(self, base, limit):
                self.o = base
                self.limit = limit

            def __call__(self, shape, dt):
                n = int(np.prod(shape)) * (4 if dt == F32 else 2)
                n = (n + 63) // 64 * 64
                off = self.o
                self.o += n
                assert self.o <= self.limit, (self.o, self.limit)
                return carve(off, shape, dt)

        KB = 1024
        ca = Alloc(0, 6 * KB)
        identb = ca([128], BF16)
        identf = ca([128], F32)
        modv = ca([2, 6, 8], F32)
        cwt = ca([4, 4], F32)
        neglam = ca([1], F32)
        gs = ca([1], F32)
        stat = ca([64], F32)
        scT = ca([8, 2], F32)
        gvec = ca([4, 8], F32)
        badaT = ca([48], F32)
        epsc = ca([1], F32)
        zprev = ca([4], F32)
        coT = carve(6 * KB, [4, NT], BF16)
        aoT = carve(42 * KB, [4, NT], BF16)
        qa = Alloc(78 * KB, 135 * KB)
        QT = qa([2, NS], BF16)
        KT = qa([2, 256 + NS], BF16)
        VA = qa([NKC, 2, 130], BF16)
        QTp = qa([2, NP], BF16)
        KTp = qa([2, NP], BF16)
        VAp = qa([4, 2, 130], BF16)
        ZB = 135 * KB

        z0 = Alloc(6 * KB, 60 * KB)
        wa = [z0([8, 512], F32), z0([8, 512], F32)]
        condT = z0([8, 2], F32)
        lamv = z0([4, 64], F32)
        lj = z0([64], F32)
        diag = [z0([128], F32), z0([128], F32)]
        gsubt = z0([1], F32)

        Wp = carve(ZB, [8, 2816], BF16)
        WpB = carve(ZB, [8, 1280], BF16)
        w1 = Alloc(ZB + 44 * KB, TOTAL)
        xt = [w1([D], F32), w1([D], F32)]
        uT = [w1([8, 512], BF16), w1([8, 512], BF16)]
        xn = w1([D], BF16)
        xn_2 = w1([D], BF16)
        w1b = Alloc(60 * KB, 78 * KB)
        w1c = Alloc(ZB + 20 * KB, ZB + 44 * KB)
        def p1work(al):
            d = {}
            d["rot"] = al([2, 512], F32)
            _t2 = al([512], F32)
            d["t2"] = [_t2, _t2]
            d["zb"] = al([514], F32)
            d["yb"] = al([512], F32)
            d["cgs"] = _t2
            d["xt2"] = al([D], F32)
            d["kst"] = al([256], F32)
            d["vst"] = al([256], F32)
            d["ckb"] = al([2, 256], BF16)
            if al is w1c:
                d["wac"] = al([8, 128], F32)
            d["hcg"] = al([4], F32)
            d["zh"] = al([4], F32)
            return d
        p1wA = p1work(w1b)
        p1wB = p1work(w1c)

        z2 = Alloc(ZB + 44 * KB, TOTAL)
        PT = [z2([2, 512], BF16) for _ in range(3)]
        AO = z2([4, 128], F32)
        otmp = z2([128], F32)
        aon = z2([4, 128], BF16)
        rz = z2([8], F32)
        junk = z2([128], F32)

        z3a = Alloc(78 * KB, ZB)
        x1bb = [z3a([4, D], F32), z3a([4, D], F32)]
        hT = z3a([NJ, 512], BF16)
        sg = [z3a([512], BF16), z3a([512], BF16)]
        woutb = carve(ZB, [8, D], BF16)
        z3 = Alloc(ZB + 16 * KB, TOTAL)
        GA = z3([D], F32)
        GF = z3([D], F32)
        diag3 = [z3([128], F32), z3([128], F32)]
        ring = [z3([2816], BF16) for _ in range(3)]
        u2T = z3([8, 512], BF16)
        fTall = z3([8, 512], F32)
        xn2 = z3([D], BF16)
        xn2b = z3([D], BF16)
        sgj = z3([2, 512], BF16)
        r_sgj = Res()

        bank = [Res(excl=True) for _ in range(8)]

        def psb(b, n=1):
            return ps[:, b:b + n, :]

        def ps_bf(b):
            return ps[:, b, :].bitcast(BF16)

        stat_i = [0]

        def newstat(n=1):
            i = stat_i[0]
            if i + n > 64:
                i = 0
            stat_i[0] = i + n
            return stat[:, i:i + n], Res()

        r_Wp = Res()
        r_WpC = Res()
        s_wp = P.new_dma_sem("d_wp")
        s_wpc = P.new_dma_sem("d_wpc")
        s_ckb = P.new_dma_sem("d_ckb")
        r_ckb_d = {}

        def load_weights(pa):
            if pa in r_ckb_d:
                return
            W = Wp if pa == 0 else WpB
            ncols = 2816 if pa == 0 else 1280
            ckb = (p1wA if pa == 0 else p1wB)["ckb"]
            wv = wA_d[pa].rearrange("(kc p) c -> p kc c", p=128)
            for kc in range(8):
                I(pool, lambda e: e.dma_start(out=W[:, kc, 0:1280], in_=wv[:, kc, 0:1280]), writes=[r_Wp], dma_sem=s_wp)
            if ncols > 1280:
                for kc in range(8):
                    I(pool, lambda e: e.dma_start(out=W[:, kc, 1280:ncols], in_=wv[:, kc, 1280:ncols]),
                      writes=[r_WpC], dma_sem=s_wpc)
            r_ckb_d[pa] = Res()
            ckv = ck_d.rearrange("(c p) f -> p c f", p=128)
            I(pool, lambda e: e.dma_start(out=ckb, in_=ckv[:, :, pa * 256:(pa + 1) * 256]), writes=[r_ckb_d[pa]],
              dma_sem=s_ckb)

        if stage >= 0.91:
            load_weights(0)
        r_small = Res()
        r_ident = r_small
        s_c = P.new_dma_sem("d_const")
        I(sp, lambda e: e.dma_start(out=identb, in_=identb_d[:, :]), writes=[r_small], dma_sem=s_c)
        I(sp, lambda e: e.dma_start(out=identf, in_=identf_d[:, :]), writes=[r_small], dma_sem=s_c)
        I(sp, lambda e: e.dma_start(out=cwt, in_=cw_d[:, :, :]), writes=[r_small], dma_sem=s_c)
        I(sp, lambda e: e.dma_start(out=gvec, in_=gvec_d[:, :, :]), writes=[r_small], dma_sem=s_c)
        I(sp, lambda e: e.dma_start(out=badaT, in_=badaT_d[:, :]), writes=[r_small], dma_sem=s_c)
        I(sp, lambda e: e.dma_start(out=condT, in_=condT_d[:, :, :]), writes=[r_small], dma_sem=s_c)
        I(sp, lambda e: e.dma_start(out=lamv, in_=lamv_d[:, :, :]), writes=[r_small], dma_sem=s_c)
        I(sp, lambda e: e.dma_start(out=gsubt, in_=gsub_d[:, :]), writes=[r_small], dma_sem=s_c)

        s_scr = P.new_dma_sem("d_scr")
        s_scr.nobarrier = True
        r_scr = Res()

        def issue_scratch():
            for j in range(NJ if stage >= 0.2 else 0):
                I(pool, lambda e, j=j: e.dma_start(out=S1_d[j], in_=F1_d[j]), writes=[r_scr], dma_sem=s_scr)
            for c in range(8 if stage >= 0.2 else 0):
                for hh in range(2):
                    I(pool, lambda e, c=c, hh=hh: e.dma_start(out=S2_d[c][:, hh * 1408:(hh + 1) * 1408],
                                                              in_=F2_d[c][:, hh * 1408:(hh + 1) * 1408]),
                      writes=[r_scr], dma_sem=s_scr)

        r_eps = Res()
        I(dve, lambda e: e.memset(epsc, EPS), writes=[r_eps])
        r_scT = Res()
        I(act, lambda e: e.activation(out=scT, in_=condT, func=AF.Silu), reads=[r_small], writes=[r_scT])
        s1, r_s1 = newstat(2)
        r_lj = Res()
        for li in range(2):
            I(dve, lambda e, li=li: e.tensor_tensor(out=lj, in0=lamv[:, 2 * li, :], in1=lamv[:, 2 * li + 1, :], op=ALU.mult),
              reads=[r_small], writes=[r_lj])
            I(dve, lambda e, li=li: e.reduce_sum(out=s1[:, li:li + 1], in_=lj, axis=mybir.AxisListType.X),
              reads=[r_lj], writes=[r_s1])
        e12, r_e12 = newstat(2)
        I(act, lambda e: e.activation(out=e12, in_=s1, func=AF.Exp), reads=[r_s1], writes=[r_e12])
        r_lam = Res()
        I(dve, lambda e: e.tensor_tensor(out=neglam, in0=e12[:, 1:2], in1=e12[:, 0:1], op=ALU.subtract),
          reads=[r_e12], writes=[r_lam])
        I(dve, lambda e: e.tensor_scalar(out=neglam, in0=neglam, scalar1=-LAMBDA_INIT, scalar2=None, op0=ALU.add),
          reads=[r_lam], writes=[r_lam])
        r_gs = Res()
        I(dve, lambda e: e.tensor_scalar(out=gs, in0=gsubt, scalar1=1.0 - LAMBDA_INIT, scalar2=None, op0=ALU.mult),
          reads=[r_small], writes=[r_gs])

        s_wa = [P.new_dma_sem("d_wa0"), P.new_dma_sem("d_wa1")]
        r_wa = [Res(), Res()]
        wada_v = wada_d.rearrange("(kc p) c -> p kc c", p=128)
        mps = ps[:, 0, 0:32].rearrange("p (a b) -> p a b", b=2)
        mps2 = ps[:, 6, 448:512].rearrange("p (a b) -> p a b", b=2)
        for cb in range(4 if stage >= 0.4 else 0):
            sl = cb % 2
            for kc2 in range(2):
                I(sp, lambda e, cb=cb, sl=sl, kc2=kc2: e.dma_start(out=wa[sl][:, kc2 * 4:(kc2 + 1) * 4, :],
                                                                   in_=wada_v[:, kc2 * 4:(kc2 + 1) * 4, cb * 512:(cb + 1) * 512]),
                  writes=[r_wa[sl]], dma_sem=s_wa[sl])
            for ch in range(4):
                cidx = cb * 4 + ch
                for kc in range(8):
                    I(pe, lambda e, sl=sl, ch=ch, kc=kc, cidx=cidx: e.matmul(
                        mps[:, cidx, :], wa[sl][:, kc, ch * 128:(ch + 1) * 128], scT[:, kc, :],
                        start=(kc == 0), stop=(kc == 7)),
                      reads=[r_wa[sl], r_scT], writes=[bank[0]], inc=(kc == 7))
        mall = ca([48, 2], F32)
        r_mall = Res()
        r_modv = Res()
        for cnd in range(2):
            I(dve, lambda e, cnd=cnd: e.tensor_tensor(out=mall[:, 0:16, cnd], in0=mps[:, :, cnd], in1=badaT[:, 0:16],
                                                      op=ALU.add),
              reads=[bank[0], r_small], writes=[r_mall])
        for cnd in range(2):
            I(dve, lambda e, cnd=cnd: e.scalar_tensor_tensor(
                out=modv[:, cnd, 0, :], in0=mall[:, 8:16, cnd], scalar=1.0, in1=gvec[:, 0, :], op0=ALU.add,
                op1=ALU.mult), reads=[r_mall, r_small], writes=[r_modv])
            I(dve, lambda e, cnd=cnd: e.tensor_copy(out=modv[:, cnd, 1, :], in_=mall[:, 0:8, cnd]),
              reads=[r_mall], writes=[r_modv])

        r_modv2 = Res()
        s_wac = P.new_dma_sem("d_wac")
        r_wac = Res()
        mod_i = [16]

        def mod_dma(wac):
            cidx = mod_i[0]
            if cidx >= 48 or stage < 0.4:
                return
            I(pool, lambda e: e.dma_start(out=wac, in_=wada_v[:, :, cidx * 128:(cidx + 1) * 128]), writes=[r_wac],
              dma_sem=s_wac)

        def mod_step(wac):
            cidx = mod_i[0]
            if cidx >= 48 or stage < 0.4:
                return
            for kc in range(8):
                I(pe, lambda e, kc=kc: e.matmul(mps2[:, cidx - 16, :], wac[:, kc, :], scT[:, kc, :],
                                                start=(kc == 0), stop=(kc == 7)),
                  reads=[r_wac, r_scT], writes=[bank[6]], inc=(kc == 7))
            mod_i[0] += 1
            mod_dma(wac)

        def mod_finish():
            if stage < 0.4:
                return
            for cnd in range(2):
                I(dve, lambda e, cnd=cnd: e.tensor_tensor(out=mall[:, 16:48, cnd], in0=mps2[:, :, cnd],
                                                          in1=badaT[:, 16:48], op=ALU.add),
                  reads=[bank[6], r_small], writes=[r_mall])
            for cnd in range(2):
                I(dve, lambda e, cnd=cnd: e.scalar_tensor_tensor(
                    out=modv[:, cnd, 3, :], in0=mall[:, 32:40, cnd], scalar=1.0, in1=gvec[:, 2, :], op0=ALU.add,
                    op1=ALU.mult), reads=[r_mall, r_small], writes=[r_modv2])
                I(dve, lambda e, cnd=cnd: e.tensor_copy(out=modv[:, cnd, 4, :], in_=mall[:, 24:32, cnd]),
                  reads=[r_mall], writes=[r_modv2])
                for (dst, gt_i, g_i) in ((2, 2, 1), (5, 5, 3)):
                    I(dve, lambda e, cnd=cnd, dst=dst, gt_i=gt_i, g_i=g_i: e.tensor_tensor(
                        out=modv[:, cnd, dst, :], in0=mall[:, gt_i * 8:(gt_i + 1) * 8, cnd], in1=gvec[:, g_i, :],
                        op=ALU.mult),
                      reads=[r_mall, r_small], writes=[r_modv2])
        P.barrier()

        r_xt = [Res(), Res()]
        s_xt = [P.new_dma_sem("d_xt0"), P.new_dma_sem("d_xt1")]
        r_xn = Res()
        tcount = [0]

        def rstd_from(src_ap, src_res, n_feat, junk_out, junk_res):
            ss, r_ss = newstat()
            if not isinstance(src_res, list):
                src_res = [src_res]
            I(act, lambda e: e.activation(out=junk_out, in_=src_ap, func=AF.Square, scale=float(n_feat) ** -0.5,
                                          accum_out=ss),
              reads=src_res, writes=[junk_res, r_ss])
            sd, r_sd = newstat()
            I(act, lambda e: e.activation(out=sd, in_=ss, func=AF.Sqrt, bias=epsc[:, 0:1]), reads=[r_ss, r_eps],
              writes=[r_sd])
            rs, r_rs = newstat()
            I(dve, lambda e: e.reciprocal(out=rs, in_=sd), reads=[r_sd], writes=[r_rs])
            return rs, r_rs

        ev_i = [0]
        ev_mode = ["alt"]

        def transposes_to(xn_ap, xn_res, tb, dst, dst_res, cnd, ai, bi, tok0):
            pv = ps_bf(tb).rearrange("p (a b) -> p a b", b=128)
            for kc in range(8):
                I(pe, lambda e, kc=kc: e.transpose(pv[:, kc, :], xn_ap[:, kc * 128:(kc + 1) * 128], identb),
                  reads=[xn_res, r_ident], writes=[bank[tb]], inc=(kc == 7))
            ev_i[0] += 1
            for kc in range(8):
                Aap = modv[:, cnd, ai, kc:kc + 1]
                Bap = modv[:, cnd, bi, kc:kc + 1]
                if ev_mode[0] != "act" and ev_i[0] % 2 == 0:
                    I(dve, lambda e, kc=kc, Aap=Aap, Bap=Bap: e.tensor_scalar(
                        out=dst[:, kc, tok0:tok0 + 128], in0=pv[:, kc, :], scalar1=Aap, scalar2=Bap,
                        op0=ALU.mult, op1=ALU.add), reads=[bank[tb], r_modv, r_modv2], writes=[dst_res])
                else:
                    I(act, lambda e, kc=kc, Aap=Aap, Bap=Bap: e.activation(
                        out=dst[:, kc, tok0:tok0 + 128], in_=pv[:, kc, :], func=AF.Identity, bias=Bap, scale=Aap),
                      reads=[bank[tb], r_modv, r_modv2], writes=[dst_res])

        xnb = [xn, xn_2]
        r_xnb = [Res(), Res()]

        xt3l = [None]
        r_xt3 = [r_xt[0], r_xt[1], Res()]
        s_xt3 = [s_xt[0], s_xt[1], P.new_dma_sem("d_xt2")]

        def xt_slot(n):
            return xt[n % 3] if n % 3 < 2 else xt3l[0]

        def prep_a0(src_ap, n):
            xs = n % 3
            I(sp, lambda e: e.dma_start(out=xt_slot(n), in_=src_ap), writes=[r_xt3[xs]], dma_sem=s_xt3[xs])

        def prep_a1(n):
            xs = n % 3
            sl = n % 2
            xv = xt_slot(n)
            rs, r_rs = rstd_from(xv, r_xt3[xs], D, xnb[sl], r_xnb[sl])
            I(dve, lambda e: e.tensor_scalar(out=xnb[sl], in0=xv, scalar1=rs, scalar2=None, op0=ALU.mult),
              reads=[r_xt3[xs], r_rs], writes=[r_xnb[sl]])
            return sl

        def prep_a(src_ap):
            n = tcount[0]
            tcount[0] += 1
            prep_a0(src_ap, n)
            return prep_a1(n)

        def prep_b(sl, cnd, dst, dst_res, tok0, tb):
            transposes_to(xnb[sl], r_xnb[sl], tb, dst, dst_res, cnd, 0, 1, tok0)

        def prep_tile(src_ap, cnd, dst, dst_res, tok0, tb):
            sl = prep_a(src_ap)
            prep_b(sl, cnd, dst, dst_res, tok0, tb)

        r_uT = [Res(), Res()]
        r_QT = {}
        r_KT = {}
        r_VA = {}
        r_co = {}
        r_ao = {}
        s_st = P.new_dma_sem("d_state")
        s_stk = P.new_dma_sem("d_statek")
        s_cva = [P.new_dma_sem("d_cva0"), P.new_dma_sem("d_cva1")]
        s_rot = P.new_dma_sem("d_rot")
        r_ones = Res()
        if stage >= 0.6:
            I(pool, lambda e: e.memset(VA[:, :, :, 128:130], 1.0), writes=[r_ones])
            I(pool, lambda e: e.memset(VAp[:, :, :, 128:130], 1.0), writes=[r_ones])

        def blk_src(b, t):
            if b < 8:
                return xs_d[b * 512 + t * 128: b * 512 + (t + 1) * 128, :]
            return xp_d[t * 128:(t + 1) * 128, :]

        def p1_pass(pa):
            ev_mode[0] = "act" if pa == 0 else "alt"
            W = Wp if pa == 0 else WpB
            ncols = 2816 if pa == 0 else 1280
            wk = p1wA if pa == 0 else p1wB
            rot, t2, zb, yb, cgs, kst, vst = wk["rot"], wk["t2"], wk["zb"], wk["yb"], wk["cgs"], wk["kst"], wk["vst"]
            r_rot, r_zb, r_yb, r_kst, r_vst = Res(), Res(), Res(), Res(), Res()
            _r = Res()
            r_t2 = [_r, _r]
            r_cgs = _r
            xt3l[0] = wk["xt2"]
            ckb = wk["ckb"]
            load_weights(pa)
            r_ckb = r_ckb_d[pa]
            cvv = cv_d.rearrange("(c p) (h e) -> p c h e", p=128, e=128)
            for c in range(2):
                r_VA[c] = Res()
                I(pool, lambda e, c=c: e.dma_start(out=VA[:, c, :, 0:128], in_=cvv[:, c, 2 * pa:2 * pa + 2, :]),
                  writes=[r_VA[c]], dma_sem=s_cva[c])
            if pa == 0:
                issue_scratch()
            pv = ps_bf(1).rearrange("p (a b) -> p a b", b=128)
            if stage < 0.92:
                P.barrier()
                return
            for c in range(2):
                for hl in range(2):
                    I(pe, lambda e, c=c, hl=hl: e.transpose(pv[:, c * 2 + hl, :], ckb[:, c, hl * 128:(hl + 1) * 128],
                                                            identb),
                      reads=[r_ckb, r_ident], writes=[bank[1]], inc=(c == 1 and hl == 1))
            r_KT[("c", 0)] = Res()
            for hl in range(2):
                I(act, lambda e, hl=hl: e.activation(
                    out=KT[:, hl, 0:256].rearrange("p (c k) -> p c k", k=128), in_=pv[:, hl:4:2, :], func=AF.Copy),
                  reads=[bank[1]], writes=[r_KT[("c", 0)]])

            pair_ring = [(2, 3), (4, 5)]
            pr_i = [0]
            r_zp = Res()
            r_hz = Res()
            hcg, zh = wk["hcg"], wk["zh"]
            if stage < 0.93:
                P.barrier()
                return
            g0 = tcount[0]

            def gsrc(g):
                return blk_src(g // 4, g % 4)

            prep_a0(gsrc(0), g0)
            for t in range(4):
                prep_a0(gsrc(t + 1), g0 + t + 1)
                sl_ = prep_a1(g0 + t)
                prep_b(sl_, 0, uT[0], r_uT[0], t * 128, (0, 7)[t % 2])
            tcount[0] = g0 + 36
            for b in range(9):
                cnd = 0 if b < 8 else 1
                sl = b % 2
                u = uT[sl]
                pending = []
                if b + 1 < 9:
                    nsl = (b + 1) % 2
                    ncnd = 0 if b + 1 < 8 else 1
                    st_ = {}

                    def mk(t, nsl=nsl, ncnd=ncnd, b=b, st_=st_):
                        def part():
                            g = 4 * (b + 1) + t
                            if t <= 3:
                                if g + 1 < 36:
                                    prep_a0(gsrc(g + 1), g0 + g + 1)
                                st_[t] = prep_a1(g0 + g)
                            if t >= 1:
                                prep_b(st_[t - 1], ncnd, uT[nsl], r_uT[nsl], (t - 1) * 128, (0, 7)[(t - 1) % 2])
                        return part
                    pending = [mk(t) for t in range(5)]
                    pending.pop(0)()
                if pa == 1 and b == 0:
                    mod_dma(wk["wac"])
                if b < 8:
                    I(sp, lambda e, b=b: e.dma_start(out=rot, in_=rot_d[b]), writes=[r_rot], dma_sem=s_rot)
                for c in range(4 if stage >= 0.94 else 0):
                    hl = c % 2
                    isk = c >= 2
                    b0, b1 = pair_ring[pr_i[0] % 2]
                    pr_i[0] += 1
                    col = c * 256
                    for kc in range(8):
                        I(pe, lambda e, kc=kc, col=col, b0=b0: e.matmul(ps[:, b0, :], W[:, kc, col:col + 128], u[:, kc, :],
                                                                       start=(kc == 0), stop=(kc == 7)),
                          reads=[r_Wp, r_uT[sl]], writes=[bank[b0]], inc=(kc == 7))
                    if b < 8:
                        for kc in range(8):
                            I(pe, lambda e, kc=kc, col=col, b1=b1: e.matmul(ps[:, b1, :], W[:, kc, col + 128:col + 256],
                                                                           u[:, kc, :], start=(kc == 0), stop=(kc == 7)),
                              reads=[r_Wp, r_uT[sl]], writes=[bank[b1]], inc=(kc == 7))
                        if isk:
                            dst = KT[:, hl, 256 + b * 512: 256 + (b + 1) * 512]
                            rr = r_KT[(hl, b)] = Res()
                        else:
                            dst = QT[:, hl, b * 512:(b + 1) * 512]
                            rr = r_QT[(hl, b)] = Res()
                        ti = pr_i[0] % 2
                        I(dve, lambda e, b0=b0: e.tensor_tensor(out=ps[:, b0, :], in0=ps[:, b0, :], in1=rot[:, 0, :],
                                                                op=ALU.mult), reads=[bank[b0], r_rot], writes=[bank[b0]])
                        I(dve, lambda e, b1=b1, ti=ti: e.tensor_tensor(out=t2[ti], in0=ps[:, b1, :], in1=rot[:, 1, :],
                                                                       op=ALU.mult),
                          reads=[bank[b1], r_rot], writes=[r_t2[ti]])
                        I(dve, lambda e, b0=b0, ti=ti, dst=dst: e.tensor_tensor(out=dst, in0=ps[:, b0, :], in1=t2[ti],
                                                                                op=ALU.add),
                          reads=[bank[b0], r_t2[ti]], writes=[rr])
                    else:
                        if isk:
                            dst = KTp[:, hl, :]
                            rr = r_KT[("p", hl)] = Res()
                        else:
                            dst = QTp[:, hl, :]
                            rr = r_QT[("p", hl)] = Res()
                        I(act, lambda e, b0=b0, dst=dst: e.activation(out=dst, in_=ps[:, b0, :], func=AF.Copy),
                          reads=[bank[b0]], writes=[rr])
                    if pending:
                        pending.pop(0)()
                    if pa == 1:
                        mod_step(wk["wac"])
                while pending:
                    pending.pop(0)()
                for t in range(4 if stage >= 0.95 else 0):
                    vb = (1, 6)[t % 2] if b < 8 else 1
                    vps = ps[:, vb, 0:256]
                    for kc in range(8):
                        I(pe, lambda e, kc=kc, t=t: e.matmul(vps, u[:, kc, t * 128:(t + 1) * 128], W[:, kc, 1024:1280],
                                                             start=(kc == 0), stop=(kc == 7)),
                          reads=[r_Wp, r_uT[sl]], writes=[bank[vb]], inc=(kc == 7))
                    vin = vps.rearrange("p (h e) -> p h e", e=128)
                    if b < 8:
                        ch = 2 + b * 4 + t
                        r_VA[ch] = Res()
                        I(act, lambda e, ch=ch, vin=vin: e.activation(out=VA[:, ch, :, 0:128], in_=vin, func=AF.Copy),
                          reads=[bank[vb], r_ones], writes=[r_VA[ch]])
                    else:
                        r_VA[("p", t)] = Res()
                        I(act, lambda e, t=t, vin=vin: e.activation(out=VAp[:, t, :, 0:128], in_=vin, func=AF.Copy),
                          reads=[bank[vb], r_ones], writes=[r_VA[("p", t)]])
                        if stage >= 0.952:
                            I(dve, lambda e: e.tensor_copy(out=vst, in_=vps), reads=[bank[vb]], writes=[r_vst])
                            I(sp, lambda e, t=t: e.dma_start(out=sv_d[t * 128:(t + 1) * 128, pa * 256:(pa + 1) * 256],
                                                             in_=vst), reads=[r_vst], dma_sem=s_st)
                        if stage < 0.954:
                            continue
                        kb = 6
                        for hl in range(2):
                            for kc in range(8):
                                I(pe, lambda e, kc=kc, t=t, hl=hl: e.matmul(
                                    ps[:, kb, hl * 128:(hl + 1) * 128], u[:, kc, t * 128:(t + 1) * 128],
                                    W[:, kc, 512 + hl * 256: 512 + hl * 256 + 128], start=(kc == 0), stop=(kc == 7)),
                                  reads=[r_Wp, r_uT[sl]], writes=[bank[kb]], inc=(kc == 7))
                        I(act, lambda e: e.activation(out=kst, in_=ps[:, kb, 0:256], func=AF.Copy), reads=[bank[kb]],
                          writes=[r_kst])
                        I(sp, lambda e, t=t: e.dma_start(out=sk_d[t * 128:(t + 1) * 128, pa * 256:(pa + 1) * 256],
                                                         in_=kst), reads=[r_kst], dma_sem=s_stk)
                if pa == 0 and stage >= 0.96:
                    if b < 7:
                        un = uT[(b + 1) % 2]
                        hzp = ps[:, 1, 384:400].rearrange("p (j g t) -> p j g t", g=2, t=2)
                        for j in range(4):
                            for gi in (1, 2):
                                col = 1280 + gi * 512 + j * 128
                                for kc in range(8):
                                    I(pe, lambda e, kc=kc, col=col, j=j, gi=gi: e.matmul(
                                        hzp[:, j, gi - 1, :], W[:, kc, col:col + 128], un[:, kc, 0:2],
                                        start=(kc == 0), stop=(kc == 7)),
                                      reads=[r_WpC, r_uT[(b + 1) % 2]], writes=[bank[1]],
                                      inc=(kc == 7 and j == 3 and gi == 2))
                        I(dve, lambda e: e.tensor_copy(out=hcg, in_=hzp[:, :, 0, 0]), reads=[bank[1]], writes=[r_hz])
                        I(dve, lambda e: e.tensor_tensor(out=zh, in0=hzp[:, :, 1, 0], in1=hcg, op=ALU.mult),
                          reads=[bank[1], r_hz], writes=[r_hz])
                    for j in range(4):
                        bb = (2, 3, 4) if j % 2 == 0 else (5, 6, 7)
                        for gi in range(3):
                            col = 1280 + gi * 512 + j * 128
                            for kc in range(8):
                                I(pe, lambda e, kc=kc, col=col, bk=bb[gi]: e.matmul(
                                    ps[:, bk, :], W[:, kc, col:col + 128], u[:, kc, :], start=(kc == 0), stop=(kc == 7)),
                                  reads=[r_WpC, r_uT[sl]], writes=[bank[bb[gi]]], inc=(kc == 7))
                        segs = [(0, 512)] if b < 8 else [(0, 256), (256, 512)]
                        I(act, lambda e, bk=bb[1]: e.activation(out=cgs, in_=ps[:, bk, :], func=AF.Copy),
                          reads=[bank[bb[1]]], writes=[r_cgs])
                        if b >= 1 and b < 8:
                            I(dve, lambda e, j=j: e.tensor_copy(out=zb[:, 0:1], in_=zprev[:, j:j + 1]), reads=[r_zp],
                              writes=[r_zb])
                        else:
                            I(dve, lambda e: e.memset(zb[:, 0:1], 0.0), writes=[r_zb])
                        I(dve, lambda e, bk=bb[2]: e.tensor_tensor(out=zb[:, 1:513], in0=ps[:, bk, :], in1=cgs,
                                                                   op=ALU.mult),
                          reads=[bank[bb[2]], r_cgs], writes=[r_zb])
                        if b < 7:
                            I(dve, lambda e, j=j: e.tensor_copy(out=zb[:, 513:514], in_=zh[:, j:j + 1]), reads=[r_hz],
                              writes=[r_zb])
                        else:
                            I(dve, lambda e: e.memset(zb[:, 513:514], 0.0), writes=[r_zb])
                        I(dve, lambda e, j=j: e.tensor_copy(out=zprev[:, j:j + 1], in_=zb[:, 512:513]), reads=[r_zb],
                          writes=[r_zp])
                        for (a0, a1) in segs:
                            n = a1 - a0
                            if len(segs) == 2:
                                pass
                            I(dve, lambda e, j=j, a0=a0, a1=a1: e.tensor_scalar(
                                out=yb[:, a0:a1], in0=zb[:, 1 + a0:1 + a1], scalar1=cwt[:, j, 1:2], scalar2=cwt[:, j, 3:4],
                                op0=ALU.mult, op1=ALU.add), reads=[r_zb, r_small], writes=[r_yb])
                            lo = a0 + (1 if (len(segs) == 2 or b == 0 or b == 8) and True else 0)
                            la = a0 + 1 if (len(segs) == 2) else a0
                            I(dve, lambda e, j=j, la=la, a1=a1: e.scalar_tensor_tensor(
                                out=yb[:, la:a1], in0=zb[:, la:a1], scalar=cwt[:, j, 0:1], in1=yb[:, la:a1],
                                op0=ALU.mult, op1=ALU.add), reads=[r_zb, r_yb, r_small], writes=[r_yb])
                            ra = a1 - 1 if (len(segs) == 2) else a1
                            I(dve, lambda e, j=j, a0=a0, ra=ra: e.scalar_tensor_tensor(
                                out=yb[:, a0:ra], in0=zb[:, a0 + 2:ra + 2], scalar=cwt[:, j, 2:3], in1=yb[:, a0:ra],
                                op0=ALU.mult, op1=ALU.add), reads=[r_zb, r_yb, r_small], writes=[r_yb])
                        tokb = b * 512
                        r_co[(j, b)] = Res()
                        I(dve, lambda e, j=j, tokb=tokb, bk=bb[0]: e.tensor_tensor(
                            out=coT[:, j, tokb:tokb + 512], in0=ps[:, bk, :], in1=yb, op=ALU.mult),
                          reads=[bank[bb[0]], r_yb], writes=[r_co[(j, b)]])
            if pa == 1:
                mod_finish()
            P.barrier()

        accS = z2([9 * 129], F32)
        r_accS = Res()

        class AttnUnit:
            def __init__(self, hl, Qv, Kv, Vv, nq, nkc, dst, dst_res, rq, rk, rv):
                self.__dict__.update(locals())
                self.nqs = nq // 128
                self.r_PT = r_PT_glob
                self.sb = [(0, 1), (2, 3)]

            def acc(self, qs, m):
                idx = qs * 2 + m
                return 4 + idx // 3, (idx % 3) * 129

            def qk(self, kc):
                b0, b1 = self.sb[kc % 2]
                Kv, Qv, nq = self.Kv, self.Qv, self.nq
                I(pe, lambda e: e.matmul(ps[:, b0, 0:nq], Kv[0:64, kc * 128:(kc + 1) * 128], Qv[0:64, :], start=True,
                                         stop=True), reads=self.rq + self.rk(kc), writes=[bank[b0]], inc=False)
                I(pe, lambda e: e.matmul(ps[:, b1, 0:nq], Kv[64:128, kc * 128:(kc + 1) * 128], Qv[64:128, :],
                                         start=True, stop=True), reads=self.rq + self.rk(kc), writes=[bank[b1]], inc=True)

            def ex(self, kc):
                b0, b1 = self.sb[kc % 2]
                pt = PT[kc % 3]
                nq = self.nq
                I(act, lambda e: e.activation(out=pt[:, :, 0:nq], in_=ps[:, b0:b0 + 2, 0:nq], func=AF.Exp,
                                              scale=ATTN_SCALE),
                  reads=[bank[b0], bank[b1]], writes=[self.r_PT[kc % 3]])

            def av(self, kc):
                pt = PT[kc % 3]
                nqs, nkc = self.nqs, self.nkc
                for qs in range(nqs):
                    for m in range(2):
                        bk, off = self.acc(qs, m)
                        first = (kc == 0 and off == 0)
                        last = (qs == nqs - 1 and m == 1)
                        I(pe, lambda e: e.matmul(
                            ps[:, bk, off:off + 129], pt[:, m, qs * 128:(qs + 1) * 128], self.Vv(kc)[:, 0:129],
                            start=first, stop=(kc == nkc - 1), skip_group_check=True),
                          reads=[self.r_PT[kc % 3]] + self.rv(kc), writes=[bank[bk]], inc=last)

            def head(self):
                self.qk(0)
                if self.nkc > 1:
                    self.qk(1)
                self.ex(0)

            def rest(self, hooks):
                nkc = self.nkc
                for kc in range(nkc):
                    if kc + 2 < nkc:
                        self.qk(kc + 2)
                    if kc + 1 < nkc:
                        self.ex(kc + 1)
                    self.av(kc)
                    for f in hooks.get(kc, []):
                        f()

            def evac_copy(self):
                nacc = self.nqs * 2
                nb = (nacc + 2) // 3
                I(dve, lambda e: e.tensor_copy(out=accS[:, 0:nb * 387].rearrange("p (a c) -> p a c", c=387),
                                               in_=ps[:, 4:4 + nb, 0:387]),
                  reads=[bank[4 + i] for i in range(nb)], writes=[r_accS])
                av_ = accS.rearrange("p (a c) -> p a c", c=129)
                r_rz = Res()
                I(dve, lambda e: e.reciprocal(out=rz[:, 0:nacc], in_=av_[:, 0:nacc, 128]), reads=[r_accS], writes=[r_rz])
                rzv = rz[:, 0:nacc].rearrange("p (q m) -> p q m", m=2)
                I(dve, lambda e: e.tensor_scalar(out=rzv[:, :, 1], in0=rzv[:, :, 1], scalar1=neglam[:, 0:1], scalar2=None,
                                                 op0=ALU.mult), reads=[r_rz, r_lam], writes=[r_rz])
                self.ss, self.r_ss = newstat(4)
                self.r_AO, r_otmp, r_junk = Res(), Res(), Res()
                for qs in range(self.nqs):
                    I(dve, lambda e, qs=qs: e.tensor_scalar(
                        out=otmp, in0=av_[:, 2 * qs, 0:128], scalar1=rz[:, 2 * qs:2 * qs + 1], scalar2=None,
                        op0=ALU.mult), reads=[r_accS, r_rz], writes=[r_otmp])
                    I(dve, lambda e, qs=qs: e.scalar_tensor_tensor(
                        out=AO[:, qs, :], in0=av_[:, 2 * qs + 1, 0:128], scalar=rz[:, 2 * qs + 1:2 * qs + 2], in1=otmp,
                        op0=ALU.mult, op1=ALU.add), reads=[r_accS, r_rz, r_otmp], writes=[self.r_AO])
                    I(dve, lambda e, qs=qs: e.tensor_tensor(out=junk, in0=AO[:, qs, :], in1=AO[:, qs, :], op=ALU.mult),
                      reads=[self.r_AO], writes=[r_junk])
                    I(dve, lambda e, qs=qs: e.reduce_sum(out=self.ss[:, qs:qs + 1], in_=junk, axis=mybir.AxisListType.X),
                      reads=[r_junk], writes=[self.r_ss])

            def evac_finish(self):
                nqs = self.nqs
                ss, r_ss = self.ss, self.r_ss
                ln_, r_ln = newstat(4)
                I(act, lambda e: e.activation(out=ln_[:, 0:nqs], in_=ss[:, 0:nqs], func=AF.Ln, bias=epsc[:, 0:1],
                                              scale=1.0 / 128), reads=[r_ss, r_eps], writes=[r_ln])
                rs_, r_rs = newstat(4)
                I(act, lambda e: e.activation(out=rs_[:, 0:nqs], in_=ln_[:, 0:nqs], func=AF.Exp, scale=-0.5),
                  reads=[r_ln], writes=[r_rs])
                r_aon = r_aon_glob
                for qs in range(nqs):
                    I(dve, lambda e, qs=qs: e.tensor_scalar(out=aon[:, qs, :], in0=AO[:, qs, :],
                                                            scalar1=rs_[:, qs:qs + 1], scalar2=None, op0=ALU.mult),
                      reads=[self.r_AO, r_rs], writes=[r_aon])
                pv = ps_bf(7).rearrange("p (a b) -> p a b", b=128)
                for qs in range(nqs):
                    I(pe, lambda e, qs=qs: e.transpose(pv[:, qs, :], aon[:, qs, :], identb), reads=[r_aon, r_ident],
                      writes=[bank[7]], inc=(qs == nqs - 1))
                dst = self.dst
                I(dve, lambda e: e.tensor_scalar(out=dst, in0=pv[:, 0:nqs, :], scalar1=gs[:, 0:1], scalar2=None,
                                                 op0=ALU.mult),
                  reads=[bank[7], r_gs], writes=[self.dst_res])

        r_PT_glob = [Res(), Res(), Res()]
        r_aon_glob = Res()

        r_GA = Res()
        r_GF = Res()
        r_diag = [Res(), Res()]
        onesf = ca([128], F32)
        r_onesf = Res()
        di_ = [0]

        def build_G(Gt, r_Gt, gi, cnd):
            if stage < 0.8:
                return
            for kc in range(8):
                dsl = di_[0] % 2
                di_[0] += 1
                I(dve, lambda e: e.tensor_scalar(
                    out=diag3[dsl], in0=identf, scalar1=modv[:, cnd, gi, kc:kc + 1], scalar2=None, op0=ALU.mult),
                  reads=[r_ident, r_modv, r_modv2], writes=[r_diag[dsl]])
                bk = kc // 4
                I(pe, lambda e: e.matmul(ps[:, bk, (kc % 4) * 128:(kc % 4 + 1) * 128], onesf, diag3[dsl],
                                         start=True, stop=True),
                  reads=[r_diag[dsl], r_onesf], writes=[bank[bk]], inc=True)
            I(act, lambda e: e.activation(out=Gt.rearrange("p (a b) -> p a b", b=512), in_=ps[:, 0:2, :], func=AF.Copy),
              reads=[bank[0], bank[1]], writes=[r_Gt])

        r_wout = Res()
        s_wo = P.new_dma_sem("d_wout")
        wov = wout_d.rearrange("(kc p) c -> p kc c", p=128)
        p3_pre = [False]

        def prefetch_p3():
            if p3_pre[0]:
                return
            p3_pre[0] = True
            for kc in range(8 if stage >= 0.9 else 0):
                I(pool, lambda e: e.dma_start(out=woutb[:, kc, :], in_=wov[:, kc, :]), writes=[r_wout], dma_sem=s_wo)
            I(dve, lambda e: e.memset(onesf, 1.0), writes=[r_onesf])
            build_G(GA, r_GA, 2, 0)
            build_G(GF, r_GF, 5, 0)

        def p2_pass(pa):
            if pa == 0 and stage >= 3:
                load_weights(1)
            if pa == 1 and stage >= 5:
                prefetch_p3()
            units = []
            for b in range(8):
                for hl in range(2):
                    h = 2 * pa + hl
                    rr = r_ao[(h, b)] = Res()
                    units.append(AttnUnit(
                        hl, QT[:, hl, b * 512:(b + 1) * 512], KT[:, hl, :],
                        lambda kc, hl=hl: VA[:, kc, hl, :], 512, NKC,
                        aoT[:, h, b * 512:(b + 1) * 512].rearrange("p (q t) -> p q t", t=128), rr,
                        [r_QT[(hl, b)]],
                        lambda kc, hl=hl: [r_KT[("c", 0)]] if kc < 2 else [r_KT[(hl, (kc - 2) // 4)]],
                        lambda kc: [r_VA[kc]]))
            for s in range(2):
                for hl in range(2):
                    h = 2 * pa + hl
                    rr = r_ao[(h, 8, s)] = Res()
                    units.append(AttnUnit(
                        hl, QTp[:, hl, s * 256:(s + 1) * 256], KTp[:, hl, s * 256:(s + 1) * 256],
                        lambda kc, hl=hl, s=s: VAp[:, 2 * s + kc, hl, :], 256, 2,
                        aoT[:, h, NS + s * 256: NS + (s + 1) * 256].rearrange("p (q t) -> p q t", t=128), rr,
                        [r_QT[("p", hl)]], lambda kc, hl=hl: [r_KT[("p", hl)]],
                        lambda kc, s=s: [r_VA[("p", 2 * s + kc)]]))
            units[0].head()
            prev = None
            for i, u in enumerate(units):
                hooks = {}
                if prev is not None:
                    hooks[min(6, u.nkc - 1)] = [prev.evac_finish]
                u.rest(hooks)
                if i + 1 < len(units):
                    units[i + 1].head()
                u.evac_copy()
                prev = u
            prev.evac_finish()
            P.barrier()

        if stage >= 0.91:
            p1_pass(0)
        if stage >= 2:
            p2_pass(0)
        if stage >= 3:
            p1_pass(1)
        if stage >= 4:
            p2_pass(1)

        ev_mode[0] = "alt"
        prefetch_p3()
        s_ring = [P.new_dma_sem("d_ring%d" % i) for i in range(3)]
        r_ring = [Res() for _ in range(3)]
        ring_i = [0]
        s_x3 = [P.new_dma_sem("d_x30"), P.new_dma_sem("d_x31")]
        s_x1 = [[P.new_dma_sem("d_x1%d_%d" % (p_, i)) for i in range(4)] for p_ in range(2)]
        r_x3 = [Res(), Res()]
        s_y = [[P.new_dma_sem("d_y%d_%d" % (p_, i)) for i in range(4)] for p_ in range(2)]
        r_y = [Res(), Res()]
        r_x1 = [[Res() for _ in range(4)] for _ in range(2)]
        r_u2 = Res()
        r_xn2 = Res()
        r_hT = [Res() for _ in range(NJ)]
        r_sg = [Res(), Res()]
        r_fT = [Res(), Res()]
        tT = [(0, 1), (2, 3)]
        fR = [(4, 5), (6, 7)]
        t_i = [0]
        f_i = [0]
        x3_i = [0]
        y_i = [0]

        def ring_load(src_ap, n):
            sl = ring_i[0] % 3
            ring_i[0] += 1
            I(sp, lambda e: e.dma_start(out=ring[sl][:, 0:n], in_=src_ap), reads=[r_scr], writes=[r_ring[sl]],
              dma_sem=s_ring[sl])
            return sl

        xn2s = [xn2, xn2b]
        r_xn2s = [Res(), Res()]
        r_fT = Res()
        nblk = 9 if stage >= 5 else 0

        def tok_of(b, t):
            return b * 512 + t * 128

        hst = {}

        def head_a(b, t):
            tok = tok_of(b, t)
            par = b % 2
            x1t = x1bb[par][:, t, :]
            rx1 = r_x1[par][t]
            I(sp, lambda e: e.dma_start(out=x1t, in_=blk_src(b, t)), writes=[rx1], dma_sem=s_x1[par][t])
            b0, b1 = tT[t_i[0] % 2]
            t_i[0] += 1
            for half in range(2):
                bk = (b0, b1)[half]
                for kc in range(8):
                    if kc < 4:
                        lhs = aoT[:, kc, tok:tok + 128]
                        rl = [r_ao[(kc, b)]] if b < 8 else [r_ao[(kc, 8, t // 2)]]
                    else:
                        lhs = coT[:, kc - 4, tok:tok + 128]
                        rl = [r_co[(kc - 4, b)]]
                    I(pe, lambda e: e.matmul(
                        ps[:, bk, :], lhs, woutb[:, kc, half * 512:(half + 1) * 512], start=(kc == 0), stop=(kc == 7)),
                      reads=rl + [r_wout], writes=[bank[bk]], inc=(kc == 7))
            mixp = ps[:, b0:b0 + 2, :]
            ss, r_ss = newstat()
            I(act, lambda e: e.activation(out=sgj, in_=mixp, func=AF.Square, scale=float(D) ** -0.5, accum_out=ss),
              reads=[bank[b0], bank[b1]], writes=[r_sgj, r_ss])
            hst[(b, t)] = dict(b0=b0, b1=b1, mixp=mixp, ss=ss, r_ss=r_ss, x1t=x1t, rx1=rx1)

        def rstd_tail(ss, r_ss):
            sd, r_sd = newstat()
            I(act, lambda e: e.activation(out=sd, in_=ss, func=AF.Sqrt, bias=epsc[:, 0:1]), reads=[r_ss, r_eps],
              writes=[r_sd])
            rs, r_rs = newstat()
            I(dve, lambda e: e.reciprocal(out=rs, in_=sd), reads=[r_sd], writes=[r_rs])
            return rs, r_rs

        def head_b(b, t):
            s = hst[(b, t)]
            rs1, r_rs1 = rstd_tail(s["ss"], s["r_ss"])
            mixp, b0, b1 = s["mixp"], s["b0"], s["b1"]
            I(dve, lambda e: e.scalar_tensor_tensor(
                out=mixp, in0=mixp, scalar=rs1, in1=GA.rearrange("p (a b) -> p a b", b=512),
                op0=ALU.mult, op1=ALU.mult), reads=[bank[b0], bank[b1], r_rs1, r_GA], writes=[bank[b0], bank[b1]])

        def head_c(b, t):
            s = hst[(b, t)]
            mixp, b0, b1, x1t, rx1 = s["mixp"], s["b0"], s["b1"], s["x1t"], s["rx1"]
            x1v = x1t.rearrange("p (a b) -> p a b", b=512)
            I(dve, lambda e: e.tensor_tensor(out=x1v, in0=mixp, in1=x1v, op=ALU.add),
              reads=[bank[b0], bank[b1], rx1], writes=[rx1])

        def head_d(b, t):
            s = hst[(b, t)]
            xs_ = t % 2
            ss, r_ss = newstat()
            I(act, lambda e: e.activation(out=xn2s[xs_], in_=s["x1t"], func=AF.Square, scale=float(D) ** -0.5,
                                          accum_out=ss), reads=[s["rx1"]], writes=[r_xn2s[xs_], r_ss])
            s["rs2"], s["r_rs2"] = rstd_tail(ss, r_ss)

        def head_e(b, t):
            s = hst[(b, t)]
            xs_ = t % 2
            I(act, lambda e: e.activation(out=xn2s[xs_], in_=s["x1t"], func=AF.Identity, scale=s["rs2"]),
              reads=[s["rx1"], s["r_rs2"]], writes=[r_xn2s[xs_]])
            head_T(b, t)

        def head_T(b, t):
            cnd = 0 if b < 8 else 1
            fb = fR[f_i[0] % 2][0]
            f_i[0] += 1
            transposes_to(xn2s[t % 2], r_xn2s[t % 2], fb, u2T, r_u2, cnd, 3, 4, t * 128)

        def ffn_in(b, hooks):
            slots = {}
            for j in range(2):
                slots[j] = ring_load(S1_d[j], 2048)
            for j in range(NJ):
                if j + 2 < NJ:
                    slots[j + 2] = ring_load(S1_d[j + 2], 2048)
                sl = slots[j]
                wv = ring[sl][:, 0:2048].rearrange("p (kc c) -> p kc c", c=256)
                bg_, bu_ = fR[f_i[0] % 2]
                f_i[0] += 1
                for kc in range(8):
                    I(pe, lambda e: e.matmul(ps[:, bg_, :], wv[:, kc, 0:128], u2T[:, kc, :],
                                             start=(kc == 0), stop=(kc == 7)),
                      reads=[r_ring[sl], r_u2], writes=[bank[bg_]], inc=(kc == 7))
                for kc in range(8):
                    I(pe, lambda e: e.matmul(ps[:, bu_, :], wv[:, kc, 128:256], u2T[:, kc, :],
                                             start=(kc == 0), stop=(kc == 7)),
                      reads=[r_ring[sl], r_u2], writes=[bank[bu_]], inc=(kc == 7))
                si = j % 2
                I(act, lambda e: e.activation(out=sg[si], in_=ps[:, bg_, :], func=AF.Silu),
                  reads=[bank[bg_]], writes=[r_sg[si]])
                I(dve, lambda e: e.tensor_tensor(out=hT[:, j, :], in0=ps[:, bu_, :], in1=sg[si], op=ALU.mult),
                  reads=[bank[bu_], r_sg[si]], writes=[r_hT[j]])
                for f in hooks.get(j, []):
                    f()

        def ffn_out(b, hooks):
            slots = {}
            for c in range(2):
                slots[c] = ring_load(S2_d[c], 2816)
            for c in range(8):
                if c + 2 < 8:
                    slots[c + 2] = ring_load(S2_d[c + 2], 2816)
                sl = slots[c]
                wv = ring[sl][:, 0:2816].rearrange("p (j n) -> p j n", n=128)
                fbk = fR[f_i[0] % 2][0]
                f_i[0] += 1
                for j in range(NJ):
                    I(pe, lambda e: e.matmul(ps[:, fbk, :], wv[:, j, :], hT[:, j, :],
                                             start=(j == 0), stop=(j == NJ - 1)),
                      reads=[r_ring[sl], r_hT[j]], writes=[bank[fbk]], inc=(j == NJ - 1))
                I(act, lambda e: e.activation(out=fTall[:, c, :], in_=ps[:, fbk, :], func=AF.Copy),
                  reads=[bank[fbk]], writes=[r_fT])
                for f in hooks.get(c, []):
                    f()

        tst = {}

        def tail_a(b, t):
            b0, b1 = tT[t_i[0] % 2]
            t_i[0] += 1
            for c in range(8):
                bk = b0 if c < 4 else b1
                I(pe, lambda e: e.transpose(ps[:, bk, (c % 4) * 128:(c % 4 + 1) * 128],
                                            fTall[:, c, t * 128:(t + 1) * 128], identf),
                  reads=[r_fT, r_ident], writes=[bank[bk]], inc=(c == 3 or c == 7))
            fp_ = ps[:, b0:b0 + 2, :]
            ss, r_ss = newstat()
            I(act, lambda e: e.activation(out=sgj, in_=fp_, func=AF.Square, scale=float(D) ** -0.5, accum_out=ss),
              reads=[bank[b0], bank[b1]], writes=[r_sgj, r_ss])
            tst[(b, t)] = dict(b0=b0, b1=b1, fp_=fp_, ss=ss, r_ss=r_ss)

        def tail_b(b, t):
            s = tst[(b, t)]
            rs3, r_rs3 = rstd_tail(s["ss"], s["r_ss"])
            fp_, b0, b1 = s["fp_"], s["b0"], s["b1"]
            I(dve, lambda e: e.scalar_tensor_tensor(
                out=fp_, in0=fp_, scalar=rs3, in1=GF.rearrange("p (a b) -> p a b", b=512),
                op0=ALU.mult, op1=ALU.mult), reads=[bank[b0], bank[b1], r_rs3, r_GF], writes=[bank[b0], bank[b1]])

        def tail_c(b, t):
            s = tst[(b, t)]
            fp_, b0, b1 = s["fp_"], s["b0"], s["b1"]
            par = b % 2
            x1t = x1bb[par][:, t, :]
            rx1 = r_x1[par][t]
            x1v = x1t.rearrange("p (a b) -> p a b", b=512)
            I(dve, lambda e: e.tensor_tensor(out=x1v, in0=fp_, in1=x1v, op=ALU.add),
              reads=[bank[b0], bank[b1], rx1], writes=[rx1])
            if b < 8:
                ydst = ys_d[b * 512 + t * 128: b * 512 + (t + 1) * 128, :]
            else:
                ydst = yp_d[t * 128:(t + 1) * 128, :]
            I(sp, lambda e: e.dma_start(out=ydst, in_=x1t), reads=[rx1], dma_sem=s_y[par][t])

        def add_hook(hk, j, f):
            hk.setdefault(j, []).append(f)

        if nblk:
            for t in range(4):
                head_a(0, t); head_b(0, t); head_c(0, t); head_d(0, t); head_e(0, t)
        for b in range(nblk):
            hk = {}
            n = b + 1
            if n == 8:
                build_G(GA, r_GA, 2, 1)
            for t in range(4):
                base = 5 * t
                if b >= 1:
                    add_hook(hk, base, lambda b=b, t=t: tail_a(b - 1, t))
                    add_hook(hk, base + 1, lambda b=b, t=t: tail_b(b - 1, t))
                    add_hook(hk, base + 2, lambda b=b, t=t: tail_c(b - 1, t))
                if n < nblk:
                    add_hook(hk, base + 2, lambda n=n, t=t: head_a(n, t))
                    add_hook(hk, base + 3, lambda n=n, t=t: head_b(n, t))
                    add_hook(hk, base + 4, lambda n=n, t=t: head_c(n, t))
                    add_hook(hk, base + 5, lambda n=n, t=t: head_d(n, t))
            ffn_in(b, hk)
            hk = {}
            if n < nblk:
                for t in range(4):
                    add_hook(hk, t, lambda n=n, t=t: head_e(n, t))
            ffn_out(b, hk)
        if nblk:
            build_G(GF, r_GF, 5, 1)
            for t in range(4):
                tail_a(nblk - 1, t); tail_b(nblk - 1, t); tail_c(nblk - 1, t)
        P.barrier()
        if dbg:
            loc = dict(locals())
            s_dbg = P.new_dma_sem("d_dbg")
            for k, (name, shape, dt) in enumerate(dbg):
                dd = nc.dram_tensor("dbg%d" % k, list(shape), dt, kind="ExternalOutput").ap()
                src_ap = eval_name(loc, name)
                I(sp, lambda e, dd=dd, src_ap=src_ap: e.dma_start(out=dd, in_=src_ap), dma_sem=s_dbg)
            P.barrier()

        with nc.Block() as block:
            @block.tensor
            def _(e):
                for f in pe.ops:
                    f(e)

            @block.scalar
            def _(e):
                for f in act.ops:
                    f(e)

            @block.vector
            def _(e):
                for f in dve.ops:
                    f(e)

            @block.gpsimd
            def _(e):
                for f in pool.ops:
                    f(e)

            @block.sync
            def _(e):
                for f in sp.ops:
                    f(e)
    return nc


_NC_CACHE = {}


def _perm_sw(cols):
    c = np.asarray(cols)
    g, r = c // 64, c % 64
    return g * 64 + ((r // 16) ^ 1) * 16 + (r % 16)


def _rot_tables():
    n_freq = 16
    inv = (1.0 / (np.float32(10000.0) ** (np.arange(n_freq, dtype=np.float32) / np.float32(n_freq)))).astype(np.float32)
    t = np.arange(NS)
    row = (t // 64).astype(np.float32)
    col = (t % 64).astype(np.float32)
    ar = (row[:, None] * inv[None, :]).astype(np.float32)
    ac = (col[:, None] * inv[None, :]).astype(np.float32)
    cr, sr, cc, sc = np.cos(ar), np.sin(ar), np.cos(ac), np.sin(ac)
    C = np.concatenate([cr, cr, cc, cc], axis=1)
    S = np.concatenate([-sr, sr, -sc, sc], axis=1)
    C = np.concatenate([C, C], axis=1).T
    S = np.concatenate([S, S], axis=1).T
    rot = np.stack([C.reshape(128, 8, 512), S.reshape(128, 8, 512)], axis=2)
    return np.ascontiguousarray(rot.transpose(1, 0, 2, 3)).astype(np.float32)


def kernel(x_prompt, x_sample, cache_k, cache_v, c, c_ctx, w_ada, b_ada,
           g_attn_pre, g_attn_post, g_ffn_pre, g_ffn_post, w_in, conv_w, conv_b,
           lambda_q1, lambda_k1, lambda_q2, lambda_k2, g_subln, w_out,
           w_ffn_in, w_ffn_out):
    f = lambda a: np.ascontiguousarray(np.asarray(a, dtype=np.float32))
    x_prompt, x_sample, cache_k, cache_v, c, c_ctx = map(f, (x_prompt, x_sample, cache_k, cache_v, c, c_ctx))
    w_ada, b_ada, w_in, w_out, w_ffn_in, w_ffn_out = map(f, (w_ada[0], b_ada[0], w_in[0], w_out[0], w_ffn_in[0],
                                                             w_ffn_out[0]))
    conv_w, conv_b = f(conv_w[0]), f(conv_b[0])
    def fm(v):
        return f(v).reshape(8, 128).T
    gvec = np.ascontiguousarray(np.stack([fm(g_attn_pre[0]), fm(g_attn_post[0]), fm(g_ffn_pre[0]), fm(g_ffn_post[0])],
                                         axis=1))
    badaT = np.ascontiguousarray(b_ada.reshape(48, 128).T)
    wA = []
    for pa in range(2):
        cols = []
        for sec in (0, 512):
            for hl in range(2):
                h = 2 * pa + hl
                base = sec + h * 128 + np.arange(128)
                cols.append(base)
                cols.append(sec + _perm_sw(h * 128 + np.arange(128)))
        cols.append(1024 + 2 * pa * 128 + np.arange(256))
        if pa == 0:
            cols.append(1536 + np.arange(1536))
        cols = np.concatenate(cols)
        wA.append(np.ascontiguousarray(w_in[:, cols]))
    cw = np.ascontiguousarray(np.stack([conv_w[0].reshape(4, 128).T, conv_w[1].reshape(4, 128).T,
                                        conv_w[2].reshape(4, 128).T, conv_b.reshape(4, 128).T], axis=2))
    lamv = np.ascontiguousarray(np.broadcast_to(
        np.stack([f(lambda_q1[0]), f(lambda_k1[0]), f(lambda_q2[0]), f(lambda_k2[0])])[None], (128, 4, 64)))
    gsub = f(g_subln[0]).reshape(128, 1)
    wi = w_ffn_in.reshape(8, 128, 2, NJ, 128)
    F1 = np.ascontiguousarray(wi.transpose(3, 1, 0, 2, 4)).reshape(NJ, 128, 8 * 256)
    wo = w_ffn_out.reshape(NJ, 128, 8, 128)
    F2 = np.ascontiguousarray(wo.transpose(2, 1, 0, 3)).reshape(8, 128, NJ * 128)
    rot = _rot_tables()
    identf = np.eye(128, dtype=np.float32)
    identb = identf.astype(ml_dtypes.bfloat16)

    n = 8
    in_maps = []
    for core in range(n):
        cond = np.stack([c[core], c_ctx])
        condT = np.ascontiguousarray(cond.reshape(2, 8, 128).transpose(2, 1, 0))
        in_maps.append({
            "xs": x_sample[core], "xp": np.ascontiguousarray(x_prompt[2 * core:2 * core + 2].reshape(NP, D)),
            "ck": np.ascontiguousarray(cache_k[core, 0].reshape(256, 512)),
            "cv": np.ascontiguousarray(cache_v[core, 0].reshape(256, 512)),
            "condT": condT, "wada": w_ada, "badaT": badaT, "gvec": gvec, "wA0": wA[0], "wA1": wA[1], "cw": cw,
            "lamv": lamv, "gsub": gsub, "wout": w_out, "F1": F1, "F2": F2, "rot": rot, "identb": identb,
            "identf": identf,
        })
    if "nc" not in _NC_CACHE:
        _NC_CACHE["nc"] = build_nc()
    nc = _NC_CACHE["nc"]
    res = run_bass_kernel_spmd(nc, in_maps, core_ids=list(range(n)))
    r = res.results
    y_sample = np.stack([r[i]["ys"] for i in range(n)]).astype(np.float32)
    y_prompt = np.concatenate([r[i]["yp"].reshape(2, 256, D) for i in range(n)]).astype(np.float32)
    state_k = np.concatenate([r[i]["sk"].reshape(2, 1, 256, 4, 128) for i in range(n)]).astype(np.float32)
    state_v = np.concatenate([r[i]["sv"].reshape(2, 1, 256, 4, 128) for i in range(n)]).astype(np.float32)
    return (y_prompt, y_sample, state_k, state_v)
```
